# Optimizing a Trainium2 kernel written in Bass

```python
import jax, jax.numpy as jnp
from jax import lax
import numpy as np

D_MODEL = 1024
BATCH = 8
SEQ = 2048
DEPTH = 2

GRID_W = 64
CTX_LEN = 256
BRANCH_WIDTH = 512
N_BRANCH = 3
LRU_WIDTH = BRANCH_WIDTH
LRU_BLOCKS = 8
LRU_BLOCK = LRU_WIDTH // LRU_BLOCKS
LRU_C = 8.0
CONV_W = 4
CONV_LEFT = 2
HG_HEADS = 4
HG_DK = 128
HG_DV = BRANCH_WIDTH // HG_HEADS
HG_QK = HG_HEADS * HG_DK
HG_WIDTH = HG_HEADS * HG_DV
HG_CHUNK = 64
NA_HEADS = 8
NA_DH = BRANCH_WIDTH // NA_HEADS
NA_WIDTH = NA_HEADS * NA_DH
WIN_R = 8
WIN_C = 16
ROPE_BASE = 10000.0
MASK_VALUE = -1e30
D_FF = 4 * D_MODEL
EPS = 1e-6
PROJ_SIZES = (LRU_WIDTH, LRU_WIDTH, HG_QK, 2 * HG_QK, HG_WIDTH, HG_WIDTH, 3 * NA_WIDTH, N_BRANCH * D_MODEL)
D_IN = sum(PROJ_SIZES)

kernel_name = 'hybrid_rglru_hgrn2_natten_dit_block'

F32 = jnp.float32


def rms_norm(x, g):
    x32 = x.astype(F32)
    y = x32 * lax.rsqrt(jnp.mean(jnp.square(x32), axis=-1, keepdims=True) + EPS)
    return (y * g.astype(F32)).astype(x.dtype)


def modulation(cond, w, b):
    m = (jax.nn.silu(cond) @ w + b)[..., None, :]
    return jnp.split(m, 6, axis=-1)


def split_projection(p):
    idx = np.cumsum(PROJ_SIZES)[:-1].tolist()
    return jnp.split(p, idx, axis=-1)


def centred_depthwise_conv(x, w, b):
    T = x.shape[1]
    xp = jnp.pad(x, ((0, 0), (CONV_LEFT, CONV_W - 1 - CONV_LEFT), (0, 0)))
    out = b
    for j in range(CONV_W):
        out = out + xp[:, j:j + T] * w[j]
    return out


def _lin_combine(left, right):
    a_l, b_l = left
    a_r, b_r = right
    return a_l * a_r, a_r * b_l + b_r


def linear_scan(a, b, h0, reverse):
    a_cum, h = lax.associative_scan(_lin_combine, (a, b), reverse=reverse, axis=1)
    if h0 is not None:
        h = h + a_cum * h0[:, None]
    return h


def rglru_branch(x_in, gate_in, conv_w, conv_b, wa, ba, wx, bx, lam, h0):
    B, T, _ = x_in.shape
    xc = centred_depthwise_conv(x_in, conv_w, conv_b).astype(F32)
    xb = xc.reshape(B, T, LRU_BLOCKS, LRU_BLOCK)
    r = jax.nn.sigmoid(jnp.einsum('btni,dnio->dbtno', xb, wa.astype(F32)).reshape(2, B, T, LRU_WIDTH) + ba.astype(F32)[:, None, None])
    i = jax.nn.sigmoid(jnp.einsum('btni,dnio->dbtno', xb, wx.astype(F32)).reshape(2, B, T, LRU_WIDTH) + bx.astype(F32)[:, None, None])
    log_a = -LRU_C * r * jax.nn.softplus(-lam.astype(F32))[:, None, None]
    a = jnp.exp(log_a)
    b = jnp.sqrt(-jnp.expm1(2.0 * log_a)) * (i * xc[None])
    h0f, h0b = (None, None) if h0 is None else (h0[0], h0[1])
    hf = linear_scan(a[0], b[0], h0f, False)
    hb = linear_scan(a[1], b[1], h0b, True)
    y = ((hf + hb) * jax.nn.gelu(gate_in.astype(F32))).astype(x_in.dtype)
    return y, jnp.stack([hf[:, -1], hb[:, 0]])


def gated_linear_chunks(q, k, v, g, s0):
    B, T, H, _ = q.shape
    n = T // HG_CHUNK

    def chunks(t):
        return t.reshape(B, n, HG_CHUNK, H, t.shape[-1]).transpose(1, 0, 3, 2, 4)

    qc, kc, vc = chunks(q), chunks(k), chunks(v)
    bc = jnp.cumsum(chunks(g), axis=3)
    incl = jnp.tril(jnp.ones((HG_CHUNK, HG_CHUNK), dtype=bool))

    def step(S, inp):
        qi, ki, vi, bi = inp
        diff = bi[:, :, :, None, :] - bi[:, :, None, :, :]
        decay = jnp.where(incl[:, :, None], jnp.exp(jnp.minimum(diff, 0.0)), 0.0)
        att = jnp.einsum('bhtd,bhsd,bhtsd->bhts', qi, ki, decay)
        o = att @ vi + jnp.einsum('bhtd,bhde->bhte', qi * jnp.exp(bi), S)
        b_last = bi[:, :, -1:, :]
        S = jnp.exp(b_last[:, :, 0, :, None]) * S + jnp.einsum('bhsd,bhse->bhde', ki * jnp.exp(b_last - bi), vi)
        return S, o

    s_fin, o = lax.scan(step, s0, (qc, kc, vc, bc))
    return o.transpose(1, 0, 3, 2, 4).reshape(B, T, H, -1), s_fin


def hgrn2_direction(q, f_raw, v, lb, s0, reverse):
    f = lb + (1.0 - lb) * jax.nn.sigmoid(f_raw)
    g = jnp.log(f)
    k = 1.0 - f
    if reverse:
        q, k, v, g = (jnp.flip(t, axis=1) for t in (q, k, v, g))
    o, s = gated_linear_chunks(q, k, v, g, s0)
    if reverse:
        o = jnp.flip(o, axis=1)
    return o, s


def hgrn2_branch(q_raw, f_raw, i_raw, og_raw, lb, norm_g, s0):
    B, T, _ = q_raw.shape
    q = jax.nn.silu(q_raw.astype(F32)).reshape(B, T, HG_HEADS, HG_DK)
    f = f_raw.astype(F32).reshape(B, T, 2, HG_HEADS, HG_DK)
    v = i_raw.astype(F32).reshape(B, T, HG_HEADS, HG_DV)
    lbh = lb.reshape(2, HG_HEADS, HG_DK)
    if s0 is None:
        s0 = jnp.zeros((2, B, HG_HEADS, HG_DK, HG_DV), F32)
    o_f, s_f = hgrn2_direction(q, f[:, :, 0], v, lbh[0], s0[0], False)
    o_b, s_b = hgrn2_direction(q, f[:, :, 1], v, lbh[1], s0[1], True)
    o = rms_norm(o_f + o_b, norm_g) * jax.nn.sigmoid(og_raw.astype(F32).reshape(B, T, HG_HEADS, HG_DV))
    return o.reshape(B, T, HG_WIDTH).astype(q_raw.dtype), jnp.stack([s_f, s_b])


def na_heads(qkv, q_g, k_g):
    B, T, _ = qkv.shape
    qkv = qkv.reshape(B, T, 3, NA_HEADS, NA_DH)
    return rms_norm(qkv[:, :, 0], q_g), rms_norm(qkv[:, :, 1], k_g), qkv[:, :, 2]


def axial_rope(x):
    T, dh = x.shape[1], x.shape[-1]
    half = dh // 2
    nf = half // 2
    t = jnp.arange(T)
    inv_freq = ROPE_BASE ** (-jnp.arange(nf, dtype=F32) / nf)

    def rotate(xh, pos):
        ang = pos.astype(F32)[:, None] * inv_freq
        cos = jnp.cos(ang)[None, :, None, :]
        sin = jnp.sin(ang)[None, :, None, :]
        x1 = xh[..., :nf].astype(F32)
        x2 = xh[..., nf:].astype(F32)
        return jnp.concatenate([x1 * cos - x2 * sin, x2 * cos + x1 * sin], axis=-1)

    out = jnp.concatenate([rotate(x[..., :half], t // GRID_W), rotate(x[..., half:], t % GRID_W)], axis=-1)
    return out.astype(x.dtype)


def neighbourhood_attention(q, k, v, q_plain, k_ctx, v_ctx, rpb):
    B, T, H, dh = q.shape
    rows = T // GRID_W
    wr = min(WIN_R, rows)
    scale = dh ** -0.5
    r = jnp.arange(rows)
    key_rows = jnp.clip(r - wr // 2, 0, rows - wr)[:, None] + jnp.arange(wr)
    col = jnp.arange(GRID_W)
    c_start = jnp.clip(col - WIN_C // 2, 0, GRID_W - WIN_C)
    col_ok = (col[None, :] >= c_start[:, None]) & (col[None, :] < c_start[:, None] + WIN_C)
    d_row = key_rows - r[:, None] + (WIN_R - 1)
    d_col = jnp.clip(col[None, :] - col[:, None], -(WIN_C - 1), WIN_C - 1) + (WIN_C - 1)
    bias = rpb.astype(F32)[:, d_row[:, None, :, None], d_col[None, :, None, :]]

    qg = q.reshape(B, rows, GRID_W, H, dh)
    kg = k.reshape(B, rows, GRID_W, H, dh)[:, key_rows]
    vg = v.reshape(B, rows, GRID_W, H, dh)[:, key_rows]
    s_loc = jnp.einsum('brqhd,brwchd->bhrqwc', qg, kg).astype(F32) * scale + bias[None]
    s_loc = jnp.where(col_ok[:, None, :], s_loc, MASK_VALUE)
    s_ctx = jnp.einsum('brqhd,blhd->bhrql', q_plain.reshape(B, rows, GRID_W, H, dh), k_ctx).astype(F32) * scale
    n_loc = wr * GRID_W
    s = jnp.concatenate([s_loc.reshape(B, H, rows, GRID_W, n_loc), s_ctx], axis=-1)
    p = jax.nn.softmax(s, axis=-1).astype(v.dtype)
    p_loc = p[..., :n_loc].reshape(B, H, rows, GRID_W, wr, GRID_W)
    p_ctx = p[..., n_loc:]
    o = jnp.einsum('bhrqwc,brwchd->brqhd', p_loc, vg) + jnp.einsum('bhrql,blhd->brqhd', p_ctx, v_ctx)
    return o.reshape(B, T, H * dh)


def context_attention(q, k, v):
    B, L, H, dh = q.shape
    s = jnp.einsum('bqhd,bkhd->bhqk', q, k).astype(F32) * dh ** -0.5
    p = jax.nn.softmax(s, axis=-1).astype(v.dtype)
    return jnp.einsum('bhqk,bkhd->bqhd', p, v).reshape(B, L, H * dh)


def merge_branches(ya, yb, yc, gate_raw, w_branch, w_out):
    ys = jnp.stack([ya, yb, yc], axis=-2)
    proj = jnp.einsum('btnw,nwd->btnd', ys, w_branch)
    g = jax.nn.sigmoid(gate_raw.reshape(gate_raw.shape[:-1] + (N_BRANCH, D_MODEL)))
    return jnp.sum(g * proj, axis=-2) @ w_out


def sq_relu_ffn(u, w1, w2):
    return jnp.square(jax.nn.relu(u @ w1)) @ w2


def setup_inputs(seed: int = 0) -> dict:
    key = jax.random.key(seed)
    ks = jax.random.split(key, 26)

    def nrm(k, shape, s):
        return jax.random.normal(k, shape, F32) * s

    a0 = jax.random.uniform(ks[15], (DEPTH, 2, LRU_WIDTH), F32, 0.9, 0.999)
    root = a0 ** (1.0 / LRU_C)
    lru_lambda = jnp.log(root) - jnp.log1p(-root)
    return {
        'x': nrm(ks[0], (BATCH, SEQ, D_MODEL), 1.0),
        'c': nrm(ks[1], (BATCH, D_MODEL), 1.0),
        'ctx': nrm(ks[2], (BATCH, CTX_LEN, D_MODEL), 1.0),
        'c_ctx': nrm(ks[3], (D_MODEL,), 1.0),
        'ada_w': nrm(ks[4], (DEPTH, D_MODEL, 6 * D_MODEL), 0.5 * D_MODEL ** -0.5),
        'ada_b': nrm(ks[5], (DEPTH, 6 * D_MODEL), 0.02),
        'norm1_g': 1.0 + nrm(ks[6], (DEPTH, D_MODEL), 0.02),
        'norm2_g': 1.0 + nrm(ks[7], (DEPTH, D_MODEL), 0.02),
        'w_in': nrm(ks[8], (DEPTH, D_MODEL, D_IN), D_MODEL ** -0.5),
        'conv_w': nrm(ks[9], (DEPTH, CONV_W, LRU_WIDTH), CONV_W ** -0.5),
        'conv_b': nrm(ks[10], (DEPTH, LRU_WIDTH), 0.02),
        'lru_wa': nrm(ks[11], (DEPTH, 2, LRU_BLOCKS, LRU_BLOCK, LRU_BLOCK), LRU_BLOCK ** -0.5),
        'lru_ba': nrm(ks[12], (DEPTH, 2, LRU_WIDTH), 0.02),
        'lru_wx': nrm(ks[13], (DEPTH, 2, LRU_BLOCKS, LRU_BLOCK, LRU_BLOCK), LRU_BLOCK ** -0.5),
        'lru_bx': nrm(ks[14], (DEPTH, 2, LRU_WIDTH), 0.02),
        'lru_lambda': lru_lambda,
        'hg_lb_logits': nrm(ks[16], (2, DEPTH, HG_QK), 0.1),
        'hg_norm_g': 1.0 + nrm(ks[17], (DEPTH, HG_DV), 0.02),
        'na_q_norm_g': 1.0 + nrm(ks[18], (DEPTH, NA_DH), 0.02),
        'na_k_norm_g': 1.0 + nrm(ks[19], (DEPTH, NA_DH), 0.02),
        'na_rpb': nrm(ks[20], (DEPTH, NA_HEADS, 2 * WIN_R - 1, 2 * WIN_C - 1), 0.1),
        'w_branch': nrm(ks[21], (DEPTH, N_BRANCH, BRANCH_WIDTH, D_MODEL), BRANCH_WIDTH ** -0.5),
        'w_out': nrm(ks[22], (DEPTH, D_MODEL, D_MODEL), D_MODEL ** -0.5),
        'ffn_w1': nrm(ks[23], (DEPTH, D_MODEL, D_FF), D_MODEL ** -0.5),
        'ffn_w2': nrm(ks[24], (DEPTH, D_FF, D_MODEL), D_FF ** -0.5),
    }


def reference(x, c, ctx, c_ctx, ada_w, ada_b, norm1_g, norm2_g, w_in, conv_w, conv_b,
              lru_wa, lru_ba, lru_wx, lru_bx, lru_lambda, hg_lb_logits, hg_norm_g,
              na_q_norm_g, na_k_norm_g, na_rpb, w_branch, w_out, ffn_w1, ffn_w2):
    lb_soft = jax.nn.softmax(hg_lb_logits.astype(F32), axis=1)
    lower_bounds = jnp.cumsum(lb_soft, axis=1) - lb_soft[:, :1]
    h, hc = x, ctx
    for l in range(DEPTH):
        ctx_needed = l < DEPTH - 1
        sh1, sc1, gt1, sh2, sc2, gt2 = modulation(c, ada_w[l], ada_b[l])
        sh1c, sc1c, gt1c, sh2c, sc2c, gt2c = modulation(c_ctx, ada_w[l], ada_b[l])
        u = rms_norm(h, norm1_g[l]) * (1.0 + sc1) + sh1
        uc = rms_norm(hc, norm1_g[l]) * (1.0 + sc1c) + sh1c
        a_x, a_g, b_q, b_f, b_i, b_o, c_qkv, m_g = split_projection(u @ w_in[l])
        a_xc, a_gc, b_qc, b_fc, b_ic, b_oc, c_qkvc, m_gc = split_projection(uc @ w_in[l])

        lru_p = (conv_w[l], conv_b[l], lru_wa[l], lru_ba[l], lru_wx[l], lru_bx[l], lru_lambda[l])
        ya_c, lru_state = rglru_branch(a_xc, a_gc, *lru_p, None)
        ya, _ = rglru_branch(a_x, a_g, *lru_p, lru_state)

        lb = lower_bounds[:, l]
        yb_c, hg_state = hgrn2_branch(b_qc, b_fc, b_ic, b_oc, lb, hg_norm_g[l], None)
        yb, _ = hgrn2_branch(b_q, b_f, b_i, b_o, lb, hg_norm_g[l], hg_state)

        q_c, k_c, v_c = na_heads(c_qkvc, na_q_norm_g[l], na_k_norm_g[l])
        q_l, k_l, v_l = na_heads(c_qkv, na_q_norm_g[l], na_k_norm_g[l])
        yc = neighbourhood_attention(axial_rope(q_l), axial_rope(k_l), v_l, q_l, k_c, v_c, na_rpb[l])

        h = h + gt1 * merge_branches(ya, yb, yc, m_g, w_branch[l], w_out[l])
        h = h + gt2 * sq_relu_ffn(rms_norm(h, norm2_g[l]) * (1.0 + sc2) + sh2, ffn_w1[l], ffn_w2[l])
        if ctx_needed:
            yc_c = context_attention(q_c, k_c, v_c)
            hc = hc + gt1c * merge_branches(ya_c, yb_c, yc_c, m_gc, w_branch[l], w_out[l])
            hc = hc + gt2c * sq_relu_ffn(rms_norm(hc, norm2_g[l]) * (1.0 + sc2c) + sh2c, ffn_w1[l], ffn_w2[l])
    return h
```

```python
import numpy as np
import concourse.bass as bass
import concourse.mybir as mybir

F32 = mybir.dt.float32
BF16 = mybir.dt.bfloat16
AF = mybir.ActivationFunctionType
ALU = mybir.AluOpType
AX = mybir.AxisListType

EPOCH = 12000


class DSem:
    __slots__ = ("sem", "cnt")

    def __init__(self, sem):
        self.sem = sem
        self.cnt = 0


class Buf:
    __slots__ = ("name", "lw", "rd", "sem", "excl")

    def __init__(self, name):
        self.name = name
        self.lw = None
        self.rd = {}
        self.sem = None
        self.excl = False


class V:
    __slots__ = ("ap", "bufs")

    def __init__(self, ap, bufs):
        self.ap = ap
        self.bufs = bufs

    def __getitem__(self, idx):
        return V(self.ap[idx], self.bufs)

    def with_ap(self, ap):
        return V(ap, self.bufs)


class Tn:
    def __init__(self, handle, name):
        self.h = handle
        self.name = name
        self.whole = Buf(name)
        self.parts = {}

    def v(self, idx=None):
        ap = self.h[:] if idx is None else self.h[idx]
        return V(ap, (self.whole,))

    def part(self, key, idx):
        b = self.parts.get(key)
        if b is None:
            b = Buf("%s/%s" % (self.name, key))
            self.parts[key] = b
        return V(self.h[idx], (b,))

    def parts_v(self, keys, idx):
        bs = []
        for k in keys:
            b = self.parts.get(k)
            if b is None:
                b = Buf("%s/%s" % (self.name, k))
                self.parts[k] = b
            bs.append(b)
        return V(self.h[idx], tuple(bs))


ENGS = ("pe", "act", "dve", "pool", "sp")


class Prog:
    def __init__(self, nc):
        self.nc = nc
        self.ops = {e: [] for e in ENGS}
        self.cnt = {e: 0 for e in ENGS}
        self.seen = {e: {} for e in ENGS}
        self.nsem = 0
        self.n_t = 0
        self.dsems = []
        self.dsem_i = 0
        self.floor = {e: {} for e in ENGS}

    NDSEM = 40

    def get_dsem(self):
        if len(self.dsems) < self.NDSEM:
            d = DSem(self.nc.alloc_semaphore("d%d" % len(self.dsems)))
            self.dsems.append(d)
            self.nsem += 1
            return d
        d = self.dsems[self.dsem_i % self.NDSEM]
        self.dsem_i += 1
        return d

    def barrier(self):
        fl = {}
        for e in ("pe", "act", "dve", "pool"):
            if self.cnt[e] > 0:
                fl[e] = self.cnt[e]
        for d in self.dsems:
            if d.cnt > 0:
                fl[d] = 16 * d.cnt
        for e in ENGS:
            f = self.floor[e]
            for k, v in fl.items():
                if f.get(k, 0) < v:
                    f[k] = v

    def sb(self, name, shape, dtype=F32):
        return Tn(self.nc.alloc_sbuf_tensor(name, list(shape), dtype), name)

    def ps(self, name, shape, dtype=F32):
        return Tn(self.nc.alloc_psum_tensor(name, list(shape), dtype), name)

    def dram(self, name, shape, dtype=F32, kind="Internal"):
        return Tn(self.nc.dram_tensor(name, list(shape), dtype, kind=kind), name)

    def emit(self, eng, fn, reads, writes, dma_buf=None):
        raw = {}
        oth = {}
        ex = [vv for vv in reads if any(b.excl for b in vv.bufs)]
        if ex:
            writes = list(writes) + ex

        def req(d, tok):
            if tok is None:
                return
            k, v = tok
            if d.get(k, 0) < v:
                d[k] = v

        for vv in reads:
            for b in vv.bufs:
                req(raw, b.lw)
        for vv in writes:
            for b in vv.bufs:
                req(oth, b.lw)
                for k, v in b.rd.items():
                    req(oth, (k, v))
        need = dict(raw)
        if self.floor[eng]:
            for k, v in self.floor[eng].items():
                if need.get(k, 0) < v:
                    need[k] = v
            self.floor[eng] = {}
        for k, v in oth.items():
            if isinstance(k, str) and k == eng and dma_buf is None:
                continue
            if need.get(k, 0) < v:
                need[k] = v
        waits = []
        seen = self.seen[eng]
        for k, v in need.items():
            if isinstance(k, str):
                if k == eng and eng == "pe":
                    continue
            else:
                v = 16 * k.cnt
            if seen.get(k, 0) >= v:
                continue
            seen[k] = v
            waits.append((k, v))
        if dma_buf is not None:
            dma_buf.cnt += 1
            tok = (dma_buf, 16 * dma_buf.cnt)
        else:
            self.cnt[eng] += 1
            tok = (eng, self.cnt[eng])
        for vv in reads:
            for b in vv.bufs:
                if b.rd.get(tok[0], 0) < tok[1]:
                    b.rd[tok[0]] = tok[1]
        for vv in writes:
            for b in vv.bufs:
                b.lw = tok
                b.rd = {}
        self.ops[eng].append((waits, fn, tok))

    def mm(self, out, lhsT, rhs, start=True, stop=True, **kw):
        self.emit("pe", lambda e: e.matmul(out.ap, lhsT.ap, rhs.ap, start=start, stop=stop, **kw),
                  [lhsT, rhs], [out])

    def transpose(self, out, in_, ident):
        self.emit("pe", lambda e: e.transpose(out.ap, in_.ap, ident.ap), [in_, ident], [out])

    def act(self, out, in_, func, bias=None, scale=None, accum_out=None, eng="act"):
        reads = [in_]
        kw = {}
        if bias is not None:
            if isinstance(bias, V):
                reads.append(bias)
                kw["bias"] = bias.ap
            else:
                kw["bias"] = bias
        if scale is not None:
            if isinstance(scale, V):
                reads.append(scale)
                kw["scale"] = scale.ap
            else:
                kw["scale"] = scale
        writes = [out]
        if accum_out is not None:
            writes.append(accum_out)
            kw["accum_out"] = accum_out.ap
        self.emit("act", lambda e: e.activation(out.ap, in_.ap, func, **kw), reads, writes)

    def tt(self, eng, out, in0, in1, op):
        self.emit(eng, lambda e: e.tensor_tensor(out.ap, in0.ap, in1.ap, op), [in0, in1], [out])

    def ts(self, eng, out, in0, s1, op0, s2=None, op1=None, accum_out=None):
        reads = [in0]
        a1 = s1
        if isinstance(s1, V):
            reads.append(s1)
            a1 = s1.ap
        a2 = s2
        if isinstance(s2, V):
            reads.append(s2)
            a2 = s2.ap
        writes = [out]
        kw = {}
        if op1 is not None:
            kw["op1"] = op1
        if accum_out is not None:
            writes.append(accum_out)
            kw["accum_out"] = accum_out.ap
        self.emit(eng, lambda e: e.tensor_scalar(out.ap, in0.ap, a1, a2, op0, **kw), reads, writes)

    def stt(self, eng, out, in0, scalar, in1, op0, op1):
        reads = [in0, in1]
        sc = scalar
        if isinstance(scalar, V):
            reads.append(scalar)
            sc = scalar.ap
        self.emit(eng, lambda e: e.scalar_tensor_tensor(out.ap, in0.ap, sc, in1.ap, op0, op1), reads, [out])

    def copy(self, eng, out, in_):
        if eng == "act":
            self.emit("act", lambda e: e.copy(out.ap, in_.ap), [in_], [out])
        else:
            self.emit(eng, lambda e: e.tensor_copy(out.ap, in_.ap), [in_], [out])

    def memset(self, eng, out, val):
        self.emit(eng, lambda e: e.memset(out.ap, val), [], [out])

    def scan(self, eng, out, d0, d1, init, op0=ALU.mult, op1=ALU.add):
        reads = [d0, d1]
        ini = init
        if isinstance(init, V):
            reads.append(init)
            ini = init.ap
        self.emit(eng, lambda e: e.tensor_tensor_scan(out.ap, d0.ap, d1.ap, ini, op0, op1), reads, [out])

    def reduce(self, eng, out, in_, op=ALU.add, axis=AX.X):
        self.emit(eng, lambda e: e.tensor_reduce(out.ap, in_.ap, axis, op), [in_], [out])

    def recip(self, out, in_):
        self.emit("dve", lambda e: e.reciprocal(out.ap, in_.ap), [in_], [out])

    def dma(self, q, out, in_, sbuf_side=None, **kw):
        side = out if sbuf_side is None else sbuf_side
        b = side.bufs[0]
        if b.sem is None:
            b.sem = self.get_dsem()
        ds = b.sem
        sem = ds.sem
        self.emit(q, lambda e: e.dma_start(out.ap, in_.ap, **kw).then_inc(sem, 16), [in_], [out], dma_buf=ds)

    def finish(self, final_waits=()):
        nc = self.nc
        waited = {e: set() for e in ENGS}
        for e in ENGS:
            for waits, fn, tok in self.ops[e]:
                for k, v in waits:
                    if isinstance(k, str):
                        waited[k].add(v)
        for k, v in final_waits:
            if isinstance(k, str):
                waited[k].add(v)
        semmap = {}
        for e in ENGS:
            vals = sorted(waited[e])
            per = {}
            for v in vals:
                ep = v // EPOCH
                per.setdefault(ep, []).append(v)
            for ep, vs in per.items():
                sem = nc.alloc_semaphore("e_%s_%d" % (e, ep))
                self.nsem += 1
                for r, v in enumerate(vs):
                    semmap[(e, v)] = (sem, r + 1)
        engobj = {"pe": "tensor", "act": "scalar", "dve": "vector", "pool": "gpsimd", "sp": "sync"}

        def run(ename):
            def body(eng):
                for waits, fn, tok in self.ops[ename]:
                    for k, v in waits:
                        if isinstance(k, str):
                            sem, val = semmap[(k, v)]
                        else:
                            sem, val = k.sem, v
                        eng.wait_ge(sem, val)
                    ins = fn(eng)
                    if isinstance(tok[0], str):
                        sv = semmap.get(tok)
                        if sv is not None:
                            ins.then_inc(sv[0], 1)
                if ename == "sp":
                    for k, v in final_waits:
                        if isinstance(k, str):
                            sem, val = semmap[(k, v)]
                        else:
                            sem, val = k.sem, 16 * k.cnt
                        eng.wait_ge(sem, val)
            return body

        with nc.Block() as block:
            block.tensor(run("pe"))
            block.scalar(run("act"))
            block.vector(run("dve"))
            block.gpsimd(run("pool"))
            block.sync(run("sp"))
        return sum(len(self.ops[e]) for e in ENGS)

from concourse.bass_utils import run_bass_kernel_spmd

NT, CT, LT = 2304, 256, 2048
NTILE = 18
TBLK = [(0, 256)] + [(256 + 512 * i, 512) for i in range(4)]
EPS = 1e-6
SLABW = 512


class Rot:
    def __init__(self, items):
        self.items = items
        self.i = 0

    def next(self):
        t = self.items[self.i % len(self.items)]
        self.i += 1
        return t


def build(n_layers=2, dbg=None):
    nc = bass.Bass("TRN2", target_bir_lowering=False)
    P = Prog(nc)
    dbg = dbg or {}

    def din(name, shape, dt=F32):
        return P.dram(name, shape, dt, kind="ExternalInput")

    x_d = din("x", [LT, 1024])
    ctx_d = din("ctx", [CT, 1024])
    cc_d = din("cc", [128, 16])
    adaw_d = din("ada_w", [2, 1024, 6144])
    adab_d = din("adabT", [2, 128, 48])
    g1_d = din("g1T", [2, 128, 8])
    g2_d = din("g2T", [2, 128, 8])
    win_d = din("w_in", [2, 1024, 8192])
    convw_d = din("convT", [2, 128, 16])
    convb_d = din("convb", [2, 128, 4])
    bda_d = din("bda", [2, 2, 4, 128, 128])
    bdx_d = din("bdx", [2, 2, 4, 128, 128])
    lruv_d = din("lruv", [2, 128, 24])
    lbl_d = din("lbl", [128, 16])
    hgg_d = din("hgg", [2, 128])
    nag_d = din("nag", [2, 128, 2])
    nab_d = din("nab", [2, 8, 21, 128, 128])
    wbr_d = din("w_branch", [2, 3, 512, 1024])
    wout_d = din("w_out", [2, 1024, 1024])
    w1_d = din("ffn_w1", [2, 1024, 4096])
    w2_d = din("ffn_w2", [2, 4096, 1024])
    ident_d = din("ident", [128, 128])
    rope_d = din("rope_cs", [2, 128, 2048])
    rotT_d = din("rotT", [128, 128])
    blk_d = din("blk", [128, 128])
    rmask_d = din("rmask", [128, 2305])
    rmask64_d = din("rmask64", [128, 2305])
    tri_d = din("tri", [2, 64, 64])
    out_d = P.dram("out", [LT, 1024], F32, kind="ExternalOutput")
    hbuf = P.dram("hbuf", [NT, 1024], F32, kind=("ExternalOutput" if dbg.get("hbuf") else "Internal"))
    modD = P.dram("modD", [2, 2, 48, 128], F32)
    dbg_out = {}

    def dbg_tensor(name, shape, dt=F32):
        dbg_out[name] = P.dram(name, shape, dt, kind="ExternalOutput")
        return dbg_out[name]

    BIGCOLS = 53200
    big = nc.alloc_sbuf_tensor("big", [128, BIGCOLS], F32)
    state = {"off": 0}

    def take(name, shape, dtype=F32):
        n = 1
        for s in shape[1:]:
            n *= s
        ncols = n if dtype == F32 else (n + 1) // 2
        off = state["off"]
        assert off + ncols <= BIGCOLS, (name, off, ncols)
        ap = big[: shape[0], off: off + ncols]
        if dtype != F32:
            ap = ap.bitcast(dtype)
            if n % 2:
                ap = ap[:, :n]
        if len(shape) == 3:
            ap = ap.rearrange("p (a b) -> p a b", a=shape[1], b=shape[2])
        elif len(shape) == 4:
            ap = ap.rearrange("p (a b c) -> p a b c", a=shape[1], b=shape[2], c=shape[3])
        state["off"] = off + ncols
        return Tn(ap, name)

    psh = nc.alloc_psum_tensor("PS", [128, 8, 512], F32)
    class PTn(Tn):
        def part(self, key, idx):
            return V(self.h[idx], (self.whole,))

    PSB = [PTn(psh[:, b, :], "ps%d" % b) for b in range(8)]
    for t_ in PSB:
        t_.whole.excl = True

    ident = take("ident", [128, 128])
    identb = take("identb", [128, 128], BF16)
    tri = take("tri", [64, 2, 64])
    cct = take("cct", [128, 16])
    sct = take("sct", [128, 8, 2])
    M0 = take("M0", [128, 48])
    M1 = take("M1", [128, 48])
    AB = take("AB", [128, 2, 2, 8])
    g1t = take("g1t", [128, 8])
    g2t = take("g2t", [128, 8])
    small = take("small", [128, 64])
    lbt = take("lbt", [128, 16])
    lbv = take("lbv", [128, 2, 2, 4])
    lruv = take("lruv", [128, 24])
    cneg = take("cneg", [128, 8])
    convw = take("convw", [128, 16])
    convb = take("convb", [128, 4])
    nag = take("nag", [128, 2])
    uT = take("uT", [128, 8, NT], BF16)
    yT = take("yT", [128, 12, NT], BF16)
    PERS = state["off"]

    P.dma("sp", ident.v(), ident_d.v())
    P.dma("sp", tri.v(), V(tri_d.h[:].rearrange("a s t -> s a t"), tri_d.v().bufs))
    P.dma("sp", cct.v(), cc_d.v())
    P.dma("sp", lbt.v(), lbl_d.v())
    P.copy("dve", identb.v(), ident.v())
    P.act(V(sct.h[:].rearrange("p a b -> p (a b)"), sct.v().bufs), cct.v(), AF.Silu)

    bank_rot = Rot([PSB[0], PSB[1], PSB[2], PSB[3]])
    q_rot = Rot(["sp", "sp"])

    def sub(tn, idx):
        return V(tn.h[idx], tn.v().bufs)

    def h_src(l, i):
        if l == 0:
            if i < 2:
                return V(ctx_d.h[i * 128:(i + 1) * 128, :], (ctx_d.whole,))
            return V(x_d.h[(i - 2) * 128:(i - 1) * 128, :], (x_d.whole,))
        return hbuf.part(i, (slice(i * 128, (i + 1) * 128), slice(None)))

    def load_w(dst_v, src_tn, rows0, nk, c0, w):
        src = src_tn[rows0: rows0 + nk * 128, c0: c0 + w].rearrange("(k p) w -> p k w", p=128)
        P.dma("pool", dst_v, V(src, ()))

    def modulation(l):
        mark = state["off"]
        wts = Rot([take("mw%d" % i, [128, 8, 256], BF16) for i in range(4)])
        sctb = take("sctb", [128, 8, 2], BF16)
        P.copy("dve", sctb.v(), sct.v())
        mps = PSB[4]
        for s in range(24):
            wt = wts.next()
            src = adaw_d.h[l, :, s * 256:(s + 1) * 256].rearrange("(k p) w -> p k w", p=128)
            P.dma("pool", wt.v(), V(src, ()))
            for jj in range(2):
                j = s * 2 + jj
                for kc in range(8):
                    P.mm(sub(mps, (slice(None), slice(j * 2, j * 2 + 2))),
                         sub(wt, (slice(None), kc, slice(jj * 128, (jj + 1) * 128))),
                         sub(sctb, (slice(None), kc, slice(None))), start=(kc == 0), stop=(kc == 7))
        adab = take("adab", [128, 48])
        P.dma("sp", adab.v(), V(adab_d.h[l], ()))
        P.dma("sp", g1t.v(), V(g1_d.h[l], ()))
        P.dma("sp", g2t.v(), V(g2_d.h[l], ()))
        mp3 = mps.h[:, 0:96].rearrange("p (j s) -> p j s", s=2)
        for s, Ms in enumerate((M0, M1)):
            P.tt("dve", Ms.v(), V(mp3[:, :, s], mps.v().bufs), adab.v(), ALU.add)
            P.stt("dve", sub(AB, (slice(None), 0, s, slice(None))), sub(Ms, (slice(None), slice(8, 16))), 1.0,
                  g1t.v(), ALU.add, ALU.mult)
            P.stt("dve", sub(AB, (slice(None), 1, s, slice(None))), sub(Ms, (slice(None), slice(32, 40))), 1.0,
                  g2t.v(), ALU.add, ALU.mult)
            tp = PSB[5]
            P.transpose(sub(tp, (slice(0, 48), slice(0, 128))), Ms.v(), ident.v())
            tsb = take("mT%d" % s, [48, 128])
            P.copy("act", tsb.v(), sub(tp, (slice(0, 48), slice(0, 128))))
            P.dma("sp", V(modD.h[l, s], (modD.whole,)), tsb.v(), sbuf_side=tsb.v())
        state["off"] = mark

    def gate_row(dst, l, s, sec):
        src = modD.h[l, s, sec * 8:(sec + 1) * 8, :].rearrange("(o a) b -> o (a b)", o=1).partition_broadcast(128)
        P.dma("sp", dst.v(), V(src[:, 0, :], (modD.whole,)))

    def norm_tile(ht, i, which, scr):
        s = 1 if i < 2 else 0
        junk, ss, hn = scr
        P.memset("dve", ss.v(), 0.0)
        P.act(junk.v(), ht.v(), AF.Square, accum_out=sub(ss, (slice(None), slice(0, 1))))
        P.ts("dve", sub(ss, (slice(None), slice(1, 2))), sub(ss, (slice(None), slice(0, 1))), 1.0 / 1024, ALU.mult, EPS, ALU.add)
        P.act(sub(ss, (slice(None), slice(2, 3))), sub(ss, (slice(None), slice(1, 2))), AF.Sqrt)
        P.recip(sub(ss, (slice(None), slice(3, 4))), sub(ss, (slice(None), slice(2, 3))))
        P.ts("dve", hn.v(), ht.v(), sub(ss, (slice(None), slice(3, 4))), ALU.mult)

    def norm_tile2(i, which, scr):
        s = 1 if i < 2 else 0
        junk, ss, hn = scr
        for half in range(2):
            pb = bank_rot.next()
            for c4 in range(4):
                c = half * 4 + c4
                P.transpose(sub(pb, (slice(None), slice(c4 * 128, (c4 + 1) * 128))),
                            sub(hn, (slice(None), slice(c * 128, (c + 1) * 128))), ident.v())
            for c4 in range(4):
                c = half * 4 + c4
                bsec = 0 if which == 0 else 24
                Ms = M1 if s else M0
                if c4 % 2 == 0:
                    P.act(sub(uT, (slice(None), c, slice(i * 128, (i + 1) * 128))),
                          sub(pb, (slice(None), slice(c4 * 128, (c4 + 1) * 128))), AF.Identity,
                          bias=sub(Ms, (slice(None), slice(bsec + c, bsec + c + 1))),
                          scale=sub(AB, (slice(None), which, s, slice(c, c + 1))))
                else:
                    P.ts("dve", sub(uT, (slice(None), c, slice(i * 128, (i + 1) * 128))),
                         sub(pb, (slice(None), slice(c4 * 128, (c4 + 1) * 128))),
                         sub(AB, (slice(None), which, s, slice(c, c + 1))), ALU.mult,
                         sub(Ms, (slice(None), slice(bsec + c, bsec + c + 1))), ALU.add)

    def norm1_phase(l):
        mark = state["off"]
        hts = Rot([take("ht%d" % i, [128, 1024]) for i in range(3)])
        scrs = Rot([(take("junk%d" % i, [128, 1024], BF16), take("ss%d" % i, [128, 4]), take("hn%d" % i, [128, 1024])) for i in range(3)])
        tiles = range(NTILE)
        prev = None
        for i in tiles:
            ht = hts.next()
            P.dma("sp", ht.v(), h_src(l, i))
            sc_ = scrs.next()
            norm_tile(ht, i, 0, sc_)
            if prev is not None:
                norm_tile2(prev[0], 0, prev[1])
            prev = (i, sc_)
        norm_tile2(prev[0], 0, prev[1])
        state["off"] = mark

    def proj_fm(wv, ncol0, blocks, consume, kn=8, rhs_of=None):
        for (t0, tl) in blocks:
            pb = bank_rot.next()
            for kc in range(kn):
                rhs = sub(uT, (slice(None), kc, slice(t0, t0 + tl))) if rhs_of is None else rhs_of(kc, t0, tl)
                P.mm(sub(pb, (slice(None), slice(0, tl))), wv[:, kc, ncol0:ncol0 + 128], rhs,
                     start=(kc == 0), stop=(kc == kn - 1))
            consume(sub(pb, (slice(None), slice(0, tl))), t0, tl)

    def mixer_A(l):
        mark = state["off"]
        W = [take("aW%d" % i, [128, 2320]) for i in range(6)]
        XB = take("aXB", [128, NT], BF16)
        slx = take("aslx", [128, 8, 512], BF16)
        slg = take("aslg", [128, 8, 512], BF16)
        bdw = Rot([take("abd%d" % i, [128, 128], BF16) for i in range(4)])
        load_w(slx.v(), win_d.h[l], 0, 8, 0, 512)
        load_w(slg.v(), win_d.h[l], 0, 8, 512, 512)
        P.dma("sp", lruv.v(), V(lruv_d.h[l], ()))
        P.dma("sp", convw.v(), V(convw_d.h[l], ()))
        P.dma("sp", convb.v(), V(convb_d.h[l], ()))
        lam = sub(lruv, (slice(None), slice(16, 24)))
        P.act(cneg.v(), lam, AF.Exp, scale=-1.0)
        P.ts("dve", cneg.v(), cneg.v(), 1.0, ALU.add)
        P.act(cneg.v(), cneg.v(), AF.Ln)
        P.ts("dve", cneg.v(), cneg.v(), -8.0, ALU.mult)
        W1, W2, W3, W4, W5, W6 = W
        W3b, W4b, W5b = [take("aWb%d" % i, [128, 2320]) for i in range(3)]
        W4g = take("aW4g", [128, 2320])
        segs = [(0, 256, 2), (256, 2304, 262 - 256)]
        for cc in range(4):
            for (a, b) in ((0, 2), (258, 262), (2310, 2312)):
                P.memset("pool", sub(W1, (slice(None), slice(a, b))), 0.0)

            def ev_x(pv, t0, tl):
                o = t0 + (2 if t0 < 256 else 6)
                P.copy("act", sub(W1, (slice(None), slice(o, o + tl))), pv)
            proj_fm(slx.v(), cc * 128, TBLK, ev_x)
            for (a, b, po) in segs:
                n = b - a
                o = sub(W2, (slice(None), slice(a, b)))

                def tap(j):
                    return sub(W1, (slice(None), slice(a + po + j - 2, a + po + j - 2 + n)))

                def cw(j):
                    return sub(convw, (slice(None), slice(cc * 4 + j, cc * 4 + j + 1)))
                P.ts("dve", o, tap(0), cw(0), ALU.mult, sub(convb, (slice(None), slice(cc, cc + 1))), ALU.add)
                for j in (1, 2, 3):
                    P.stt("dve", o, tap(j), cw(j), o, ALU.mult, ALU.add)
            P.copy("act", XB.v(), sub(W2, (slice(None), slice(0, NT))))
            full = (slice(None), slice(0, NT))

            def rv(tn, a_, b_):
                return V(tn.h[:, a_:b_][:, ::-1], tn.v().bufs)

            def dir_steps(d):
                R3, R4, R5 = (W3, W4, W5) if d == 0 else (W3b, W4b, W5b)
                wa = bdw.next()
                wx = bdw.next()
                ba = sub(lruv, (slice(None), slice(0 + d * 4 + cc, 0 + d * 4 + cc + 1)))
                bx = sub(lruv, (slice(None), slice(8 + d * 4 + cc, 8 + d * 4 + cc + 1)))
                cn = sub(cneg, (slice(None), slice(d * 4 + cc, d * 4 + cc + 1)))

                def s0():
                    P.dma("pool", wa.v(), V(bda_d.h[l, d, cc], ()))
                    P.dma("pool", wx.v(), V(bdx_d.h[l, d, cc], ()))
                    for (t0, tl) in TBLK:
                        pb = bank_rot.next()
                        P.mm(sub(pb, (slice(None), slice(0, tl))), wa.v(), sub(XB, (slice(None), slice(t0, t0 + tl))))
                        P.act(sub(R3, (slice(None), slice(t0, t0 + tl))), sub(pb, (slice(None), slice(0, tl))), AF.Sigmoid, bias=ba)
                        pb = bank_rot.next()
                        P.mm(sub(pb, (slice(None), slice(0, tl))), wx.v(), sub(XB, (slice(None), slice(t0, t0 + tl))))
                        P.act(sub(R5, (slice(None), slice(t0, t0 + tl))), sub(pb, (slice(None), slice(0, tl))), AF.Sigmoid, bias=bx)

                def s7():
                    if d == 0:
                        P.scan("dve", sub(W6, full), sub(R3, full), sub(R5, full), 0.0)
                    else:
                        P.scan("dve", rv(W2, 0, 256), rv(R3, 0, 256), rv(R5, 0, 256), 0.0)
                        P.scan("dve", rv(W2, 256, NT), rv(R3, 256, NT), rv(R5, 256, NT), sub(W2, (slice(None), slice(0, 1))))
                return [
                    s0,
                    lambda: P.act(sub(R3, full), sub(R3, full), AF.Exp, scale=cn),
                    lambda: P.act(sub(R4, full), sub(R3, full), AF.Square),
                    lambda: P.ts("dve", sub(R4, full), sub(R4, full), -1.0, ALU.mult, 1.0, ALU.add),
                    lambda: P.act(sub(R4, full), sub(R4, full), AF.Sqrt),
                    lambda: P.tt("dve", sub(R5, full), sub(R5, full), sub(W2, full), ALU.mult),
                    lambda: P.tt("dve", sub(R5, full), sub(R5, full), sub(R4, full), ALU.mult),
                    s7,
                ]
            st_ = [dir_steps(0), dir_steps(1)]

            def ev_g(pv, t0, tl):
                P.copy("act", sub(W1, (slice(None), slice(t0, t0 + tl))), pv)
            gsteps = [
                lambda: None,
                lambda: proj_fm(slg.v(), cc * 128, TBLK, ev_g),
                lambda: P.act(sub(W4g, full), sub(W1, full), AF.Square),
                lambda: P.ts("dve", sub(W4g, full), sub(W4g, full), 0.044715, ALU.mult, 1.0, ALU.add),
                lambda: P.tt("pool", sub(W4g, full), sub(W4g, full), sub(W1, full), ALU.mult),
                lambda: P.act(sub(W4g, full), sub(W4g, full), AF.Sigmoid, scale=1.5957691216057308),
                lambda: P.tt("pool", sub(W4g, full), sub(W4g, full), sub(W1, full), ALU.mult),
                lambda: None,
            ]
            for k_ in range(8):
                for d in range(2):
                    st_[d][k_]()
                gsteps[k_]()
            P.tt("dve", sub(W6, full), sub(W6, full), sub(W2, full), ALU.add)
            P.tt("dve", sub(yT, (slice(None), cc, slice(None))), sub(W6, full), sub(W4g, full), ALU.mult)
        state["off"] = mark

    def mixer_B(l):
        mark = state["off"]
        CH = 32 if l == 0 else 64
        NCH = NT // CH
        bank_rot.items = [PSB[0], PSB[1]]
        QS = take("bQS", [128, NT])
        V64 = take("bV64", [CH, NCH, 128], BF16)
        O64 = take("bO64", [CH, NCH, 128])
        RM = take("bRM", [128, 513])
        sl = take("bsl", [128, 8, 512], BF16)
        gN = take("bgN", [CH, 128])
        DEC = take("bDEC", [128, 2, NCH])
        Sst = [take("bS%d" % d, [128, 128]) for d in range(2)]
        SB3 = [[take("bSB%d%d" % (d, i), [128, 128], BF16) for i in range(3)] for d in range(2)]
        tmps = dict(T1=take("bT1", [128, 512]), KK=take("bKK", [128, 512]), B=take("bB", [128, 512]),
                    T2=take("bT2", [128, 512]), KHT=take("bKHT", [128, 512], BF16))
        tmps["T3"] = tmps["T1"]
        if CH == 64:
            tmps1 = dict(T1=take("bT1b", [128, 512]), KK=take("bKKb", [128, 512]), B=take("bBb", [128, 512]),
                         T2=take("bT2b", [128, 512]), KHT=take("bKHTb", [128, 512], BF16))
            tmps1["T3"] = tmps1["T1"]
        else:
            tmps1 = tmps
        tmpsD = [tmps, tmps1]
        AMd = [take("bAM%d" % d, [CH, 2, CH], BF16) for d in range(2)]

        def blkbufs(d, tag):
            d_ = dict(tmpsD[d])
            d_.update(QH=take("bQH" + tag, [128, 512], BF16), QT=take("bQT" + tag, [128, 512], BF16),
                      KT=take("bKT" + tag, [128, 512], BF16),
                      KH64=take("bKH64" + tag, [CH, 512 // CH, 128], BF16), AM=AMd[d])
            return d_
        BB = [Rot([blkbufs(d, "%d%d" % (d, i)) for i in range(2)]) for d in range(2)]
        fin = dict(ss=take("bfss", [CH, 4, NCH]))
        on1 = Tn(tmps["T2"].h[0:CH, 0:512].rearrange("p (a b) -> p a b", a=4), "bfon")
        on1.whole = tmps["T2"].whole
        fin["on"] = on1
        on2 = Tn(tmps["KK"].h[0:CH, 0:512].rearrange("p (a b) -> p a b", a=4), "bfon2")
        on2.whole = tmps["KK"].whole
        fin["on2"] = on2
        P.dma("sp", RM.v(), V((rmask_d if CH == 32 else rmask64_d).h[:, 0:513], ()))
        P.dma("sp", gN.v(), V(hgg_d.h[l:l + 1, :].partition_broadcast(CH)[:, 0, :], ()))
        lb4 = lbt.h[:].rearrange("p (d l h) -> p d l h", d=2, l=2, h=4)
        sm = small.h[:].rearrange("p (a d h) -> p a d h", a=8, d=2, h=4)
        sv = small.v().bufs
        P.tt("dve", V(sm[:, 0], sv), V(lb4[:, :, 0, :], lbt.v().bufs), V(lb4[:, :, 1, :], lbt.v().bufs), ALU.subtract)
        P.act(V(sm[:, 1], sv), V(sm[:, 0], sv), AF.Sigmoid)
        P.act(V(sm[:, 2], sv), V(sm[:, 0], sv), AF.Sigmoid, scale=-1.0)
        if l == 0:
            P.tt("dve", V(lbv.h[:, 0], lbv.v().bufs), V(sm[:, 1], sv), V(sm[:, 1], sv), ALU.subtract)
        else:
            P.tt("dve", V(sm[:, 3], sv), V(sm[:, 1], sv), V(sm[:, 2], sv), ALU.add)
            P.tt("dve", V(lbv.h[:, 0], lbv.v().bufs), V(sm[:, 3], sv), V(sm[:, 1], sv), ALU.subtract)
        P.ts("dve", V(lbv.h[:, 1], lbv.v().bufs), V(lbv.h[:, 0], lbv.v().bufs), -1.0, ALU.mult, 1.0, ALU.add)

        for h in range(4):
            slot = (slice(None), slice(None), slice(256, 384))
            slotB = (slice(None), slice(None), slice(384, 512))

            def ld_f(hh_):
                for j, c0 in enumerate((1536, 2048)):
                    load_w(sub(sl, (slice(None), slice(None), slice(j * 128, (j + 1) * 128))), win_d.h[l], 0, 8, c0 + hh_ * 128, 128)
            if h == 0:
                ld_f(0)
                load_w(sub(sl, slot), win_d.h[l], 0, 8, 1024, 128)
                load_w(sub(sl, slotB), win_d.h[l], 0, 8, 2560, 128)
            def ev_q(pv, t0, tl):
                P.act(sub(QS, (slice(None), slice(t0, t0 + tl))), pv, AF.Silu)
            proj_fm(sl.v(), 256, TBLK, ev_q)
            load_w(sub(sl, slot), win_d.h[l], 0, 8, 3072 + h * 128, 128)
            P.memset("pool", V(O64.h[:], tuple(O64.part(g_, (slice(None), g_, slice(None))).bufs[0] for g_ in range(NCH))), 0.0)
            for d in range(2):
                P.memset("pool", Sst[d].v(), 0.0)
                P.memset("pool", SB3[d][0].v(), 0.0)
            blocks = {0: TBLK, 1: [TBLK[0]] + TBLK[:0:-1]}
            stepc = [0, 0]
            def prep(d, t0, tl, bb):
                nch = tl // CH
                T1, KK, B_, T2, T3 = bb["T1"], bb["KK"], bb["B"], bb["T2"], bb["T3"]
                sl_ = (slice(None), slice(0, tl))
                lbs = V(lbv.h[:, 0, d, h:h + 1], lbv.v().bufs)
                oml = V(lbv.h[:, 1, d, h:h + 1], lbv.v().bufs)

                def ev_f(pv, t0_, tl_):
                    P.act(sub(T1, sl_), pv, AF.Sigmoid)
                proj_fm(sl.v(), 128 * d, [(t0, tl)], ev_f)
                yield
                P.ts("dve", sub(T1, sl_), sub(T1, sl_), oml, ALU.mult, lbs, ALU.add)
                yield
                P.ts("pool", sub(KK, sl_), sub(T1, sl_), -1.0, ALU.mult, 1.0, ALU.add)
                yield
                P.act(sub(T1, sl_), sub(T1, sl_), AF.Ln)
                yield
                if d == 0:
                    P.scan("dve", sub(B_, sl_), sub(RM, (slice(None), slice(0, tl))), sub(T1, sl_), 0.0)
                    yield
                    ilast, imid = CH - 1, CH // 2 - 1
                else:
                    P.scan("dve", V(B_.h[:, 0:tl][:, ::-1], B_.v().bufs),
                           V(RM.h[:, 1:tl + 1][:, ::-1], RM.v().bufs),
                           V(T1.h[:, 0:tl][:, ::-1], T1.v().bufs), 0.0)
                    yield
                    ilast, imid = 0, CH // 2
                B3 = B_.h[:, 0:tl].rearrange("p (c t) -> p c t", t=CH)
                bbf = B_.v().bufs

                def v3(tn):
                    return V(tn.h[:, 0:tl].rearrange("p (c t) -> p c t", t=CH), tn.v().bufs)
                P.act(sub(T2, sl_), sub(B_, sl_), AF.Exp)
                yield
                c0 = t0 // CH
                P.copy("pool", V(DEC.h[:, d, c0:c0 + nch], DEC.v().bufs),
                       V(T2.h[:, 0:tl].rearrange("p (c t) -> p c t", t=CH)[:, :, ilast], T2.v().bufs))
                yield
                P.tt("pool", sub(bb["QH"], sl_), sub(QS, (slice(None), slice(t0, t0 + tl))), sub(T2, sl_), ALU.mult)
                yield
                P.tt("dve", v3(T2), V(B3, bbf), V(B3[:, :, imid:imid + 1].broadcast_to([128, nch, CH]), bbf), ALU.subtract)
                yield
                P.act(sub(T3, sl_), sub(T2, sl_), AF.Exp)
                yield
                P.tt("pool", sub(bb["QT"], sl_), sub(QS, (slice(None), slice(t0, t0 + tl))), sub(T3, sl_), ALU.mult)
                yield
                P.act(sub(T3, sl_), sub(T2, sl_), AF.Exp, scale=-1.0)
                yield
                P.tt("pool", sub(bb["KT"], sl_), sub(KK, sl_), sub(T3, sl_), ALU.mult)
                yield
                P.tt("dve", v3(T2), V(B3[:, :, ilast:ilast + 1].broadcast_to([128, nch, CH]), bbf), V(B3, bbf), ALU.subtract)
                yield
                P.act(sub(T3, sl_), sub(T2, sl_), AF.Exp)
                yield
                P.tt("pool", sub(bb["KHT"], sl_), sub(KK, sl_), sub(T3, sl_), ALU.mult)
                yield
                for _ in range(18 if CH == 64 else 0):
                    yield
                for g4 in range(nch // 4):
                    tp = bank_rot.next()
                    tpb = V(tp.h[0:CH, :].bitcast(BF16), tp.v().bufs)
                    for q4 in range(4):
                        n = g4 * 4 + q4
                        P.transpose(tpb[:, q4 * 128:(q4 + 1) * 128], sub(bb["KHT"], (slice(None), slice(n * CH, (n + 1) * CH))), identb.v())
                        yield
                    P.copy("act", V(bb["KH64"].h[:, g4 * 4:(g4 + 1) * 4, :].rearrange("p a b -> p (a b)"), bb["KH64"].v().bufs),
                           tpb[:, 0:512])
                    yield


            import itertools
            binfo0 = [None, None]
            g0s = []
            for d in range(2):
                t0_, tl_ = blocks[d][0]
                bb_ = BB[d].next()
                g0s.append(prep(d, t0_, tl_, bb_))
                binfo0[d] = (bb_, t0_, tl_, tl_ // CH, t0_ // CH)
            gen0 = itertools.chain(*g0s)
            for g4 in range(NCH // 4):
                pb = bank_rot.next()
                for q4 in range(4):
                    n = g4 * 4 + q4
                    for kc in range(8):
                        P.mm(sub(pb, (slice(0, CH), slice(q4 * 128, (q4 + 1) * 128))),
                             sub(uT, (slice(None), kc, slice(n * CH, (n + 1) * CH))),
                             sub(sl, (slice(None), kc, slice(384, 512))), start=(kc == 0), stop=(kc == 7))
                P.copy("act", V(V64.h[:, g4 * 4:(g4 + 1) * 4, :].rearrange("p a b -> p (a b)"), V64.v().bufs),
                       sub(pb, (slice(0, CH), slice(0, 512))))
                for _ in range(5):
                    next(gen0, None)
            if h < 3:
                load_w(sub(sl, slotB), win_d.h[l], 0, 8, 2560 + (h + 1) * 128, 128)
            for _ in gen0:
                pass
            pending_gen = None
            nxt_info = None
            for bi in range(5):
                if bi == 0:
                    binfo = binfo0
                else:
                    for _ in pending_gen:
                        pass
                    binfo = nxt_info
                if bi < 4:
                    nxt_info = [None, None]
                    gens = []
                    for d in range(2):
                        t0, tl = blocks[d][bi + 1]
                        bb = BB[d].next()
                        gens.append(prep(d, t0, tl, bb))
                        nxt_info[d] = (bb, t0, tl, tl // CH, t0 // CH)
                    if CH == 64:
                        def _rr(a_, b_):
                            for xa, xb in itertools.zip_longest(a_, b_):
                                yield
                        pending_gen = _rr(gens[0], gens[1])
                    else:
                        pending_gen = itertools.chain(*gens)
                else:
                    pending_gen = iter(())
                pend = [None, None]

                def o_part(d, item):
                    bb_, n_, gn_, par_, sbi_ = item
                    cs_ = slice(n_ * CH, (n_ + 1) * CH)
                    pO = sub(PSB[(7, 2)[par_]], (slice(0, CH), slice(d * 128, (d + 1) * 128)))
                    am_ = bb_["AM"].part(par_, (slice(None), par_, slice(None)))
                    vch_ = sub(V64, (slice(None), gn_, slice(None)))
                    P.mm(pO, am_, vch_, start=True, stop=False)
                    P.mm(pO, sub(bb_["QH"], (slice(None), cs_)), SB3[d][sbi_].v(), start=False, stop=True)
                    och = O64.part(gn_, (slice(None), gn_, slice(None)))
                    P.tt("dve", och, och, pO, ALU.add)

                nsteps = binfo[0][3]
                for st in range(nsteps):
                    for d in range(2):
                        bb, t0, tl, nch, c0 = binfo[d]
                        n = st if d == 0 else nch - 1 - st
                        gn = c0 + n
                        k = stepc[d]
                        stepc[d] += 1
                        par = (k + d) % 2
                        cs = slice(n * CH, (n + 1) * CH)
                        pA = sub(PSB[3 + par], (slice(0, CH), slice(d * CH, (d + 1) * CH)))
                        pS = sub(PSB[5 + par], (slice(None), slice(d * 128, (d + 1) * 128)))
                        P.mm(pA, sub(bb["KT"], (slice(None), cs)), sub(bb["QT"], (slice(None), cs)))
                        am = bb["AM"].part(par, (slice(None), par, slice(None)))
                        P.tt("dve", am, pA, sub(tri, (slice(0, CH), d, slice(0, CH))), ALU.mult)
                        vch = sub(V64, (slice(None), gn, slice(None)))
                        P.mm(pS, sub(bb["KH64"], (slice(None), n, slice(None))), vch)
                        if pend[d] is not None:
                            o_part(d, pend[d])
                        pend[d] = (bb, n, gn, par, k % 3)
                        P.stt("dve", Sst[d].v(), Sst[d].v(), V(DEC.h[:, d, gn:gn + 1], DEC.v().bufs), pS, ALU.mult, ALU.add)
                        P.copy("act", SB3[d][(k + 1) % 3].v(), Sst[d].v())
                    for _ in range(6 if CH == 32 else 8):
                        next(pending_gen, None)
                for d in range(2):
                    if pend[d] is not None:
                        o_part(d, pend[d])
                        pend[d] = None
            if h < 3:
                ld_f(h + 1)

            def ev_og(pv, t0, tl):
                P.act(sub(QS, (slice(None), slice(t0, t0 + tl))), pv, AF.Sigmoid)
            proj_fm(sl.v(), 256, TBLK, ev_og)
            if h < 3:
                load_w(sub(sl, slot), win_d.h[l], 0, 8, 1024 + (h + 1) * 128, 128)
            ss = fin["ss"]
            for g4 in range(NCH // 4):
                o4 = O64.parts_v([g4 * 4 + i_ for i_ in range(4)], (slice(None), slice(g4 * 4, g4 * 4 + 4), slice(None)))
                sqb = fin["on"] if g4 % 2 == 0 else fin["on2"]
                P.act(sqb.v(), o4, AF.Square)
                P.reduce("dve", sub(ss, (slice(None), 0, slice(g4 * 4, g4 * 4 + 4))), sqb.v())
            P.ts("dve", sub(ss, (slice(None), 1, slice(None))), sub(ss, (slice(None), 0, slice(None))), 1.0 / 128, ALU.mult, EPS, ALU.add)
            P.act(sub(ss, (slice(None), 2, slice(None))), sub(ss, (slice(None), 1, slice(None))), AF.Sqrt)
            P.recip(sub(ss, (slice(None), 3, slice(None))), sub(ss, (slice(None), 2, slice(None))))
            for g4 in range(NCH // 4):
                o4 = O64.parts_v([g4 * 4 + i_ for i_ in range(4)], (slice(None), slice(g4 * 4, g4 * 4 + 4), slice(None)))
                onb = fin["on"] if g4 % 2 == 0 else fin["on2"]
                P.tt("dve", onb.v(), o4, V(ss.h[:, 3, g4 * 4:g4 * 4 + 4].unsqueeze(2).broadcast_to([CH, 4, 128]), ss.v().bufs), ALU.mult)
                P.tt("pool", onb.v(), onb.v(), V(gN.h[:].unsqueeze(1).broadcast_to([CH, 4, 128]), gN.v().bufs), ALU.mult)
                tp = bank_rot.next()
                for q4 in range(4):
                    P.transpose(sub(tp, (slice(None), slice(q4 * CH, (q4 + 1) * CH))), sub(onb, (slice(None), q4, slice(None))),
                                sub(ident, (slice(0, CH), slice(0, CH))))
                tsl = slice(g4 * 4 * CH, (g4 + 1) * 4 * CH)
                P.tt("dve", sub(yT, (slice(None), 4 + h, tsl)), sub(tp, (slice(None), slice(0, 4 * CH))), sub(QS, (slice(None), tsl)), ALU.mult)
        bank_rot.items = [PSB[0], PSB[1], PSB[2], PSB[3]]
        state["off"] = mark

    def pair_tiles(p):
        r = 2 * p
        if r in (0, 2):
            return [(j, 5 + (r // 2) * 4 + j) for j in range(4)]
        if r in (28, 30):
            return [(12 + j, 13 + ((r - 28) // 2) * 4 + j) for j in range(4)]
        return [((r - 4) // 2 + j, j) for j in range(5)]

    def mixer_C(l, do_ctx):
        mark = state["off"]
        blk = take("blk", [128, 128])
        rotT = take("rotT", [128, 128])
        P.dma("sp", blk.v(), blk_d.v())
        P.dma("sp", rotT.v(), rotT_d.v())
        blkb = take("blkb", [128, 128], BF16)
        P.copy("dve", blkb.v(), blk.v())
        slq = take("cslq", [128, 8, 512], BF16)
        slk = take("cslk", [128, 8, 512], BF16)
        slv = take("cslv", [128, 8, 512], BF16)
        KTr = take("cKTr", [128, NT], BF16)
        QTr = take("cQTr", [128, NT], BF16)
        QTp = take("cQTp", [128, NT], BF16)
        VA = take("cVA", [128, 18, 2, 65], BF16)
        NABf = Rot([take("cNABf%d" % i, [128, 7, 128]) for i in range(2)])
        NAB = take("cNAB", [128, 2, 21, 128], BF16)
        def mk_tb(i):
            d_ = dict(X=take("cX%d" % i, [128, 512]), SQ=take("cSQ%d" % i, [128, 512]), RS=take("cRS%d" % i, [128, 512]),
                      XN=take("cXN%d" % i, [128, 512]), CS=take("cCS%d" % i, [128, 2, 512]), SQb=take("cSQb%d" % i, [128, 512], BF16))
            d_["T"] = d_["SQ"]
            d_["U"] = d_["RS"]
            return d_
        tb = Rot([mk_tb(i) for i in range(3)])
        PT = Rot([take("cPT%d" % i, [128, 7, 128], BF16) for i in range(4)])
        scsets = Rot([(PSB[6], PSB[7]), (PSB[0], PSB[1]), (PSB[2], PSB[3])])
        PTc = take("cPTc", [128, 2, 256], BF16)
        fin = Rot([dict(rec=take("cfr%d" % i, [128, 2]), y=take("cfy%d" % i, [128, 2, 64])) for i in range(2)])
        load_w(slq.v(), win_d.h[l], 0, 8, 3584, 512)
        load_w(slk.v(), win_d.h[l], 0, 8, 4096, 512)
        load_w(slv.v(), win_d.h[l], 0, 8, 4608, 512)
        P.dma("sp", nag.v(), V(nag_d.h[l], ()))
        P.memset("pool", V(VA.h[:, :, :, 64:65], VA.v().bufs), 1.0)
        for c in range(4):
            for i in range(NTILE):
                pb = bank_rot.next()
                for kc in range(8):
                    P.mm(sub(pb, (slice(None), slice(0, 128))), sub(uT, (slice(None), kc, slice(i * 128, (i + 1) * 128))),
                         sub(slv, (slice(None), kc, slice(c * 128, (c + 1) * 128))), start=(kc == 0), stop=(kc == 7))
                P.copy("act", V(VA.h[:, i, :, 0:64], VA.v().bufs),
                       V(pb.h[:, 0:128].rearrange("p (a b) -> p a b", a=2), pb.v().bufs))
            items = [(which, slw, t0, tl) for which, slw in ((0, slq), (1, slk)) for (t0, tl) in TBLK]
            live = {}

            def st1(i):
                which, slw, t0, tl = items[i]
                t = tb.next()
                live[i] = t
                sl_ = (slice(None), slice(0, tl))

                def ev(pv, t0_, tl_):
                    P.copy("act", sub(t["X"], sl_), pv)
                proj_fm(slw.v(), c * 128, [(t0, tl)], ev)
                P.act(sub(t["SQb"], sl_), sub(t["X"], sl_), AF.Square)
                if t0 >= 256:
                    P.dma("sp", t["CS"].v(), V(rope_d.h[:, :, t0 - 256:t0 - 256 + 512].rearrange("a p t -> p a t"), ()))

            def st2(i):
                which, slw, t0, tl = items[i]
                t = live[i]
                sl_ = (slice(None), slice(0, tl))
                gsc = sub(nag, (slice(None), slice(which, which + 1)))
                pb = bank_rot.next()
                P.mm(sub(pb, sl_), blkb.v(), sub(t["SQb"], sl_))
                P.ts("dve", sub(t["RS"], sl_), sub(pb, sl_), EPS, ALU.add)
                P.act(sub(t["RS"], sl_), sub(t["RS"], sl_), AF.Sqrt)
                P.recip(sub(t["RS"], sl_), sub(t["RS"], sl_))
                P.stt("dve", sub(t["XN"], sl_), sub(t["X"], sl_), gsc, sub(t["RS"], sl_), ALU.mult, ALU.mult)
                dstp = QTp if which == 0 else KTr
                if which == 0 or t0 < 256:
                    P.copy("act", sub(dstp, (slice(None), slice(t0, t0 + tl))), sub(t["XN"], sl_))

            def st3(i):
                which, slw, t0, tl = items[i]
                t = live.pop(i)
                sl_ = (slice(None), slice(0, tl))
                if t0 >= 256:
                    pb2 = bank_rot.next()
                    P.mm(sub(pb2, sl_), rotT.v(), sub(t["XN"], sl_))
                    P.tt("dve", sub(t["T"], sl_), sub(pb2, sl_), sub(t["CS"], (slice(None), 1, slice(None))), ALU.mult)
                    P.tt("pool", sub(t["U"], sl_), sub(t["XN"], sl_), sub(t["CS"], (slice(None), 0, slice(None))), ALU.mult)
                    dstr = QTr if which == 0 else KTr
                    P.tt("dve", sub(dstr, (slice(None), slice(t0, t0 + tl))), sub(t["T"], sl_), sub(t["U"], sl_), ALU.add)
            ni = len(items)
            for i in range(ni + 2):
                if i < ni:
                    st1(i)
                if 0 <= i - 1 < ni:
                    st2(i - 1)
                if 0 <= i - 2 < ni:
                    st3(i - 2)
            for hh in range(2):
                for g3 in range(3):
                    nf = NABf.next()
                    P.dma("sp", nf.v(), V(nab_d.h[l, 2 * c + hh, g3 * 7:(g3 + 1) * 7].rearrange("m k q -> k m q"), ()))
                    P.act(sub(NAB, (slice(None), hh, slice(g3 * 7, (g3 + 1) * 7), slice(None))), nf.v(), AF.Exp)
            aitems = [(p, hh) for p in range(16) for hh in range(2)]
            alive = {}
            fins = {}

            def obank(p):
                return PSB[5] if p % 2 == 0 else PSB[4]

            def a_scores(i):
                p, hh = aitems[i]
                tiles = pair_tiles(p)
                q0 = 256 + p * 128
                hs = slice(hh * 64, (hh + 1) * 64)
                scA, scB = scsets.next()

                def scv(j):
                    bk = scA if j < 4 else scB
                    jj = j % 4
                    return sub(bk, (slice(None), slice(jj * 128, (jj + 1) * 128)))
                nl = len(tiles)
                for j, (kt, mi) in enumerate(tiles):
                    k0 = 256 + kt * 128
                    P.mm(scv(j), sub(KTr, (hs, slice(k0, k0 + 128))), sub(QTr, (hs, slice(q0, q0 + 128))))
                for j in range(2):
                    P.mm(scv(nl + j), sub(KTr, (hs, slice(j * 128, (j + 1) * 128))), sub(QTp, (hs, slice(q0, q0 + 128))))
                pt = PT.next()
                n1 = min(4, nl + 2)
                P.act(V(pt.h[:, 0:n1, :].rearrange("p a b -> p (a b)"), pt.v().bufs), sub(scA, (slice(None), slice(0, n1 * 128))), AF.Exp, scale=0.125)
                n2 = nl + 2 - n1
                if n2 > 0:
                    P.act(V(pt.h[:, 4:4 + n2, :].rearrange("p a b -> p (a b)"), pt.v().bufs), sub(scB, (slice(None), slice(0, n2 * 128))), AF.Exp, scale=0.125)
                mi0 = tiles[0][1]
                P.tt("pool", sub(pt, (slice(None), slice(0, nl), slice(None))), sub(pt, (slice(None), slice(0, nl), slice(None))),
                     sub(NAB, (slice(None), hh, slice(mi0, mi0 + nl), slice(None))), ALU.mult)
                alive[i] = (pt, tiles, nl)

            def a_pv(i):
                p, hh = aitems[i]
                pt, tiles, nl = alive.pop(i)
                oacc_v = sub(obank(p), (slice(None), slice(0, 130)))
                ov = oacc_v[:, hh * 65:hh * 65 + 65]
                for j in range(nl + 2):
                    vt = (2 + tiles[j][0]) if j < nl else (j - nl)
                    P.mm(ov, sub(pt, (slice(None), j, slice(None))), sub(VA, (slice(None), vt, hh, slice(None))),
                         start=(j == 0), stop=(j == nl + 1))
                if hh == 1:
                    f = fin.next()
                    fins[p] = f
                    o3 = oacc_v.ap.rearrange("p (a b) -> p a b", a=2)
                    P.recip(f["rec"].v(), V(o3[:, :, 64], oacc_v.bufs))
                    P.tt("dve", f["y"].v(), V(o3[:, :, 0:64], oacc_v.bufs),
                         V(f["rec"].h[:].unsqueeze(2).broadcast_to([128, 2, 64]), f["rec"].v().bufs), ALU.mult)

            def a_fin(p):
                f = fins.pop(p)
                q0 = 256 + p * 128
                tpv = sub(obank(p), (slice(None), slice(256, 384)))
                P.transpose(tpv, V(f["y"].h[:].rearrange("p a b -> p (a b)"), f["y"].v().bufs), ident.v())
                P.copy("act", sub(yT, (slice(None), 8 + c, slice(q0, q0 + 128))), tpv)
            na = len(aitems)
            a_scores(0)
            a_scores(1)
            for i in range(na):
                if i + 2 < na:
                    a_scores(i + 2)
                a_pv(i)
                p, hh = aitems[i]
                if hh == 1 and p >= 1:
                    a_fin(p - 1)
            a_fin(15)
            if do_ctx:
                for hh in range(2):
                    hs = slice(hh * 64, (hh + 1) * 64)
                    scA = PSB[6]
                    for j in range(2):
                        P.mm(sub(scA, (slice(None), slice(j * 256, (j + 1) * 256))), sub(KTr, (hs, slice(j * 128, (j + 1) * 128))),
                             sub(QTp, (hs, slice(0, 256))))
                    P.act(V(PTc.h[:].rearrange("p a b -> p (a b)"), PTc.v().bufs), sub(scA, (slice(None), slice(0, 512))), AF.Exp, scale=0.125)
                    for i in range(2):
                        oacc = PSB[5]
                        ov = sub(oacc, (slice(None), slice((i * 2 + hh) * 65, (i * 2 + hh) * 65 + 65)))
                        for j in range(2):
                            P.mm(ov, sub(PTc, (slice(None), j, slice(i * 128, (i + 1) * 128))), sub(VA, (slice(None), j, hh, slice(None))),
                                 start=(j == 0), stop=(j == 1))
                for i in range(2):
                    f = fin.next()
                    oacc = PSB[5]
                    o3 = oacc.h[:, i * 130:(i + 1) * 130].rearrange("p (a b) -> p a b", a=2)
                    P.recip(f["rec"].v(), V(o3[:, :, 64], oacc.v().bufs))
                    P.tt("dve", f["y"].v(), V(o3[:, :, 0:64], oacc.v().bufs),
                         V(f["rec"].h[:].unsqueeze(2).broadcast_to([128, 2, 64]), f["rec"].v().bufs), ALU.mult)
                    tp = PSB[4]
                    P.transpose(sub(tp, (slice(None), slice(0, 128))), V(f["y"].h[:].rearrange("p a b -> p (a b)"), f["y"].v().bufs), ident.v())
                    P.copy("act", sub(yT, (slice(None), 8 + c, slice(i * 128, (i + 1) * 128))), sub(tp, (slice(None), slice(0, 128))))
        state["off"] = mark

    def merge_phase(l, blocks, tiles):
        mark = state["off"]
        MG = take("mMG", [128, 8, NT], BF16)
        wg = Rot([take("mwg%d" % i, [128, 8, 384], BF16) for i in range(2)])
        wb = Rot([take("mwb%d" % i, [128, 12, 128], BF16) for i in range(2)])
        sg = Rot([take("msg%d" % i, [128, 512]) for i in range(3)])
        acc = Rot([take("macc%d" % i, [128, 512]) for i in range(2)])
        def ld_merge(j):
            g__ = wg.next()
            b__ = wb.next()
            for n in range(3):
                load_w(sub(g__, (slice(None), slice(None), slice(n * 128, (n + 1) * 128))), win_d.h[l], 0, 8, 5120 + n * 1024 + j * 128, 128)
                load_w(sub(b__, (slice(None), slice(n * 4, (n + 1) * 4), slice(None))), wbr_d.h[l, n], 0, 4, j * 128, 128)
            return g__, b__
        nxtw = ld_merge(0)
        for j in range(8):
            g_, b_ = nxtw
            if j < 7:
                nxtw = ld_merge(j + 1)
            for (t0, tl) in blocks:
                sl_ = (slice(None), slice(0, tl))
                a_ = acc.next()
                for n in range(3):
                    s_ = sg.next()

                    def ev(pv, t0_, tl_):
                        P.act(sub(s_, sl_), pv, AF.Sigmoid)
                    proj_fm(g_.v(), n * 128, [(t0, tl)], ev)
                    pb = bank_rot.next()
                    for kc in range(4):
                        P.mm(sub(pb, sl_), sub(b_, (slice(None), n * 4 + kc, slice(None))), sub(yT, (slice(None), n * 4 + kc, slice(t0, t0 + tl))),
                             start=(kc == 0), stop=(kc == 3))
                    if n == 0:
                        P.tt("dve", sub(a_, sl_), sub(pb, sl_), sub(s_, sl_), ALU.mult)
                    else:
                        P.tt("dve", sub(s_, sl_), sub(pb, sl_), sub(s_, sl_), ALU.mult)
                        if n == 1:
                            P.tt("pool", sub(a_, sl_), sub(a_, sl_), sub(s_, sl_), ALU.add)
                        else:
                            P.tt("pool", sub(MG, (slice(None), j, slice(t0, t0 + tl))), sub(a_, sl_), sub(s_, sl_), ALU.add)
        P.barrier()
        state["off"] = mark + (8 * NT) // 2
        WO = take("mWO", [128, 8, 1024], BF16)
        load_w(WO.v(), wout_d.h[l], 0, 8, 0, 1024)
        GT = [take("mGT%d" % s, [128, 1024]) for s in range(2)]
        gate_row(GT[0], l, 0, 2)
        if tiles[0] < 2:
            gate_row(GT[1], l, 1, 2)
        hts = Rot([take("mht%d" % i, [128, 1024]) for i in range(3)])
        scrs = Rot([(take("mjunk%d" % i, [128, 1024], BF16), take("mss%d" % i, [128, 4]), take("mhn%d" % i, [128, 1024])) for i in range(3)])
        prevn = None
        tmp = Rot([take("mtmp%d" % i, [128, 512]) for i in range(2)])
        for i in tiles:
            s = 1 if i < 2 else 0
            ht = hts.next()
            P.dma("sp", ht.v(), h_src(l, i))
            for fb in range(2):
                pb = bank_rot.next()
                for kc in range(8):
                    P.mm(pb.v(), sub(MG, (slice(None), kc, slice(i * 128, (i + 1) * 128))), sub(WO, (slice(None), kc, slice(fb * 512, (fb + 1) * 512))),
                         start=(kc == 0), stop=(kc == 7))
                t_ = tmp.next()
                P.tt("dve", t_.v(), pb.v(), sub(GT[s], (slice(None), slice(fb * 512, (fb + 1) * 512))), ALU.mult)
                P.tt("pool", sub(ht, (slice(None), slice(fb * 512, (fb + 1) * 512))), sub(ht, (slice(None), slice(fb * 512, (fb + 1) * 512))), t_.v(), ALU.add)
            P.dma("sp", hbuf.part(i, (slice(i * 128, (i + 1) * 128), slice(None))), ht.v(), sbuf_side=ht.v())
            sc_ = scrs.next()
            norm_tile(ht, i, 1, sc_)
            if prevn is not None:
                norm_tile2(prevn[0], 1, prevn[1])
            prevn = (i, sc_)
        norm_tile2(prevn[0], 1, prevn[1])
        state["off"] = mark

    def ffn_phase(l, tiles, last):
        mark = state["off"]
        state["off"] = PERS - (12 * NT) // 2
        t_first = tiles[0]
        ntl = len(tiles)
        ACC = take("fACC", [128, ntl, 1024])
        HID = take("fHID", [128, 4, NT], BF16)
        W1s = Rot([take("fW1%d" % i, [128, 8, 512], BF16) for i in range(2)])
        W2s = Rot([take("fW2%d" % i, [128, 4, 1024], BF16) for i in range(2)])
        rl = Rot([take("frl%d" % i, [128, 512]) for i in range(3)])
        GT = [take("fGT%d" % s, [128, 1024]) for s in range(2)]
        hts = Rot([take("fht%d" % i, [128, 1024]) for i in range(5)])
        blocks = [b for b in TBLK if b[0] >= t_first * 128]
        def ld_ffn(g):
            w1_ = W1s.next()
            w2_ = W2s.next()
            load_w(w1_.v(), w1_d.h[l], 0, 8, g * 512, 512)
            load_w(w2_.v(), w2_d.h[l], g * 512, 4, 0, 1024)
            return w1_, w2_
        nxtf = ld_ffn(0)
        for g in range(8):
            w1, w2 = nxtf
            if g < 7:
                nxtf = ld_ffn(g + 1)
            for q in range(4):
                def ev(pv, t0, tl):
                    r_ = rl.next()
                    P.act(sub(r_, (slice(None), slice(0, tl))), pv, AF.Relu)
                    P.tt("pool", sub(HID, (slice(None), q, slice(t0, t0 + tl))), sub(r_, (slice(None), slice(0, tl))), sub(r_, (slice(None), slice(0, tl))), ALU.mult)
                proj_fm(w1.v(), q * 128, blocks, ev)
            for ti, i in enumerate(tiles):
                for fb in range(2):
                    pb = bank_rot.next()
                    for q in range(4):
                        P.mm(pb.v(), sub(HID, (slice(None), q, slice(i * 128, (i + 1) * 128))), sub(w2, (slice(None), q, slice(fb * 512, (fb + 1) * 512))),
                             start=(q == 0), stop=(q == 3))
                    av = sub(ACC, (slice(None), ti, slice(fb * 512, (fb + 1) * 512)))
                    if g == 0:
                        P.copy("act", av, pb.v())
                    else:
                        P.tt("dve", av, av, pb.v(), ALU.add)
        gate_row(GT[0], l, 0, 5)
        if t_first < 2:
            gate_row(GT[1], l, 1, 5)
        for ti, i in enumerate(tiles):
            s = 1 if i < 2 else 0
            ht = hts.next()
            hv = hbuf.part(i, (slice(i * 128, (i + 1) * 128), slice(None)))
            P.dma("sp", ht.v(), hv)
            av = sub(ACC, (slice(None), ti, slice(None)))
            P.tt("dve", av, av, GT[s].v(), ALU.mult)
            P.tt("dve", ht.v(), ht.v(), av, ALU.add)
            if last:
                P.dma("sp", V(out_d.h[(i - 2) * 128:(i - 1) * 128, :], (out_d.whole,)), ht.v(), sbuf_side=ht.v())
            else:
                P.dma("sp", hv, ht.v(), sbuf_side=ht.v())
        state["off"] = mark

    def dump_T(name, tn, nchunk):
        d = dbg_tensor(name, [128, nchunk, NT], BF16)
        P.dma("sp", d.v(), tn, sbuf_side=tn)

    stop_after = dbg.get("stop_after")
    for l in range(n_layers):
        last = (l == n_layers - 1)
        modulation(l)
        P.barrier()
        norm1_phase(l)
        P.barrier()
        if dbg.get("u") == l:
            dump_T("dbg_uT", uT.v(), 8)
        if stop_after == ("norm1", l):
            break
        if "A" in dbg.get("mixers", "ABC"):
            mixer_A(l)
            P.barrier()
        if "B" in dbg.get("mixers", "ABC"):
            mixer_B(l)
            P.barrier()
        if "C" in dbg.get("mixers", "ABC"):
            mixer_C(l, do_ctx=not last)
            P.barrier()
        if dbg.get("y") == l:
            dump_T("dbg_yT", yT.v(), 12)
        if stop_after == ("mix", l):
            break
        if last:
            blocks, tiles = TBLK[1:], list(range(2, NTILE))
        else:
            blocks, tiles = TBLK, list(range(NTILE))
        merge_phase(l, blocks, tiles)
        P.barrier()
        ffn_phase(l, tiles, last)
        P.barrier()

    final = []
    seenb = set()
    for e in ENGS:
        for waits, fn, tok in P.ops[e]:
            if not isinstance(tok[0], str) and id(tok[0]) not in seenb:
                seenb.add(id(tok[0]))
                final.append(tok)
    n = P.finish(final_waits=final)
    return nc, n, list(dbg_out.keys())


def _consts():
    c = {}
    c["ident"] = np.eye(128, dtype=np.float32)
    t = np.arange(2048)
    inv = (10000.0 ** (-np.arange(16, dtype=np.float32) / 16)).astype(np.float32)
    cs = np.zeros((2, 128, 2048), np.float32)
    for pp in range(128):
        d = pp % 64
        pos = (t // 64) if d < 32 else (t % 64)
        ang = pos.astype(np.float32) * inv[d % 16]
        cs[0, pp] = np.cos(ang)
        cs[1, pp] = np.sin(ang)
    c["rope_cs"] = cs
    R = np.zeros((128, 128), np.float32)
    for m in range(128):
        i = m % 32
        if i < 16:
            R[m, m + 16] = -1.0
        else:
            R[m, m - 16] = 1.0
    c["rotT"] = np.ascontiguousarray(R.T)
    b = np.zeros((128, 128), np.float32)
    b[:64, :64] = 1.0 / 64
    b[64:, 64:] = 1.0 / 64
    c["blk"] = b
    rm = np.ones((128, 2305), np.float32)
    rm[:, ::32] = 0.0
    c["rmask"] = rm
    rm64 = np.ones((128, 2305), np.float32)
    rm64[:, ::64] = 0.0
    c["rmask64"] = rm64
    s = np.arange(64)[:, None]
    tt = np.arange(64)[None, :]
    c["tri"] = np.stack([(s <= tt), (s >= tt)]).astype(np.float32)
    return c


def _nab(rpb):
    out = np.full((2, 8, 21, 128, 128), -1e30, np.float32)
    a = np.arange(128)[:, None] // 64
    qc = np.arange(128)[:, None] % 64
    b = np.arange(128)[None, :] // 64
    kc = np.arange(128)[None, :] % 64
    cstart = np.clip(qc - 8, 0, 48)
    col_ok = (kc >= cstart) & (kc < cstart + 16)
    dcol = np.clip(kc - qc, -15, 15) + 15
    classes = [(8, [(2 + j, j) for j in range(5)])]
    classes.append((0, [(j, 5 + j) for j in range(4)]))
    classes.append((2, [(j, 9 + j) for j in range(4)]))
    classes.append((28, [(12 + j, 13 + j) for j in range(4)]))
    classes.append((30, [(12 + j, 17 + j) for j in range(4)]))
    for r, lst in classes:
        for kt, mi in lst:
            qrow = r + a
            krow = 2 * kt + b
            kr0 = np.clip(qrow - 4, 0, 24)
            valid = (krow >= kr0) & (krow < kr0 + 8) & col_ok
            drow = np.clip(krow - qrow + 7, 0, 14)
            drow, dcol_b = np.broadcast_arrays(drow, dcol)
            g = rpb[:, :, drow, dcol_b]
            out[:, :, mi] = np.swapaxes(np.where(valid[None, None], g, np.float32(-1e30)), -1, -2)
    return out


def _prep(inp):
    f = np.float32
    shared = {}
    for k in ("ada_w", "w_in", "w_branch", "w_out", "ffn_w1", "ffn_w2"):
        shared[k] = np.ascontiguousarray(inp[k], dtype=f)
    shared["adabT"] = np.ascontiguousarray(inp["ada_b"].reshape(2, 48, 128).transpose(0, 2, 1))
    shared["g1T"] = np.ascontiguousarray(inp["norm1_g"].reshape(2, 8, 128).transpose(0, 2, 1))
    shared["g2T"] = np.ascontiguousarray(inp["norm2_g"].reshape(2, 8, 128).transpose(0, 2, 1))
    shared["convT"] = np.ascontiguousarray(inp["conv_w"].reshape(2, 4, 4, 128).transpose(0, 3, 2, 1).reshape(2, 128, 16))
    shared["convb"] = np.ascontiguousarray(inp["conv_b"].reshape(2, 4, 128).transpose(0, 2, 1))
    for name, key in (("bda", "lru_wa"), ("bdx", "lru_wx")):
        w = inp[key]
        bd = np.zeros((2, 2, 4, 128, 128), f)
        for cch in range(4):
            for hf in range(2):
                bd[:, :, cch, hf * 64:(hf + 1) * 64, hf * 64:(hf + 1) * 64] = w[:, :, 2 * cch + hf]
        shared[name] = bd
    lv = np.stack([inp["lru_ba"], inp["lru_bx"], inp["lru_lambda"]], axis=1)
    shared["lruv"] = np.ascontiguousarray(lv.reshape(2, 3, 2, 4, 128).transpose(0, 4, 1, 2, 3).reshape(2, 128, 24))
    lb = inp["hg_lb_logits"]
    shared["lbl"] = np.ascontiguousarray(lb.reshape(2, 2, 4, 128).transpose(3, 0, 1, 2).reshape(128, 16))
    shared["hgg"] = np.ascontiguousarray(inp["hg_norm_g"], dtype=f)
    ng = np.stack([np.tile(inp["na_q_norm_g"], (1, 2)), np.tile(inp["na_k_norm_g"], (1, 2))], axis=2)
    shared["nag"] = np.ascontiguousarray(ng, dtype=f)
    shared["nab"] = _nab(np.asarray(inp["na_rpb"], dtype=f))
    shared.update(_consts())
    maps = []
    for b in range(8):
        m = dict(shared)
        m["x"] = np.ascontiguousarray(inp["x"][b], dtype=f)
        m["ctx"] = np.ascontiguousarray(inp["ctx"][b], dtype=f)
        cc = np.stack([inp["c"][b].reshape(8, 128).T, inp["c_ctx"].reshape(8, 128).T], axis=2)
        m["cc"] = np.ascontiguousarray(cc.reshape(128, 16), dtype=f)
        maps.append(m)
    return maps


_CACHE = {}


def kernel(**inputs):
    inp = {k: np.asarray(v) for k, v in inputs.items()}
    if "nc" not in _CACHE:
        _CACHE["nc"] = build()[0]
    nc = _CACHE["nc"]
    maps = _prep(inp)
    res = run_bass_kernel_spmd(nc, maps, core_ids=list(range(8)))
    out = np.stack([np.asarray(r["out"], dtype=np.float32) for r in res.results], axis=0)
    return out
```

```python
import numpy as np
import concourse.bass as bass
import concourse.mybir as mybir

F32 = mybir.dt.float32
BF16 = mybir.dt.bfloat16
AF = mybir.ActivationFunctionType
ALU = mybir.AluOpType
AX = mybir.AxisListType

EPOCH = 12000


class DSem:
    __slots__ = ("sem", "cnt")

    def __init__(self, sem):
        self.sem = sem
        self.cnt = 0


class Buf:
    __slots__ = ("name", "lw", "rd", "sem", "excl")

    def __init__(self, name):
        self.name = name
        self.lw = None
        self.rd = {}
        self.sem = None
        self.excl = False


class V:
    __slots__ = ("ap", "bufs")

    def __init__(self, ap, bufs):
        self.ap = ap
        self.bufs = bufs

    def __getitem__(self, idx):
        return V(self.ap[idx], self.bufs)

    def with_ap(self, ap):
        return V(ap, self.bufs)


class Tn:
    def __init__(self, handle, name):
        self.h = handle
        self.name = name
        self.whole = Buf(name)
        self.parts = {}

    def v(self, idx=None):
        ap = self.h[:] if idx is None else self.h[idx]
        return V(ap, (self.whole,))

    def part(self, key, idx):
        b = self.parts.get(key)
        if b is None:
            b = Buf("%s/%s" % (self.name, key))
            self.parts[key] = b
        return V(self.h[idx], (b,))

    def parts_v(self, keys, idx):
        bs = []
        for k in keys:
            b = self.parts.get(k)
            if b is None:
                b = Buf("%s/%s" % (self.name, k))
                self.parts[k] = b
            bs.append(b)
        return V(self.h[idx], tuple(bs))


ENGS = ("pe", "act", "dve", "pool", "sp")


class Prog:
    def __init__(self, nc):
        self.nc = nc
        self.ops = {e: [] for e in ENGS}
        self.cnt = {e: 0 for e in ENGS}
        self.seen = {e: {} for e in ENGS}
        self.nsem = 0
        self.n_t = 0
        self.dsems = []
        self.dsem_i = 0
        self.floor = {e: {} for e in ENGS}

    NDSEM = 40

    def get_dsem(self):
        if len(self.dsems) < self.NDSEM:
            d = DSem(self.nc.alloc_semaphore("d%d" % len(self.dsems)))
            self.dsems.append(d)
            self.nsem += 1
            return d
        d = self.dsems[self.dsem_i % self.NDSEM]
        self.dsem_i += 1
        return d

    def barrier(self):
        fl = {}
        for e in ("pe", "act", "dve", "pool"):
            if self.cnt[e] > 0:
                fl[e] = self.cnt[e]
        for d in self.dsems:
            if d.cnt > 0:
                fl[d] = 16 * d.cnt
        for e in ENGS:
            f = self.floor[e]
            for k, v in fl.items():
                if f.get(k, 0) < v:
                    f[k] = v

    def sb(self, name, shape, dtype=F32):
        return Tn(self.nc.alloc_sbuf_tensor(name, list(shape), dtype), name)

    def ps(self, name, shape, dtype=F32):
        return Tn(self.nc.alloc_psum_tensor(name, list(shape), dtype), name)

    def dram(self, name, shape, dtype=F32, kind="Internal"):
        return Tn(self.nc.dram_tensor(name, list(shape), dtype, kind=kind), name)

    def emit(self, eng, fn, reads, writes, dma_buf=None):
        raw = {}
        oth = {}
        ex = [vv for vv in reads if any(b.excl for b in vv.bufs)]
        if ex:
            writes = list(writes) + ex

        def req(d, tok):
            if tok is None:
                return
            k, v = tok
            if d.get(k, 0) < v:
                d[k] = v

        for vv in reads:
            for b in vv.bufs:
                req(raw, b.lw)
        for vv in writes:
            for b in vv.bufs:
                req(oth, b.lw)
                for k, v in b.rd.items():
                    req(oth, (k, v))
        need = dict(raw)
        if self.floor[eng]:
            for k, v in self.floor[eng].items():
                if need.get(k, 0) < v:
                    need[k] = v
            self.floor[eng] = {}
        for k, v in oth.items():
            if isinstance(k, str) and k == eng and dma_buf is None:
                continue
            if need.get(k, 0) < v:
                need[k] = v
        waits = []
        seen = self.seen[eng]
        for k, v in need.items():
            if isinstance(k, str):
                if k == eng and eng == "pe":
                    continue
            else:
                v = 16 * k.cnt
            if seen.get(k, 0) >= v:
                continue
            seen[k] = v
            waits.append((k, v))
        if dma_buf is not None:
            dma_buf.cnt += 1
            tok = (dma_buf, 16 * dma_buf.cnt)
        else:
            self.cnt[eng] += 1
            tok = (eng, self.cnt[eng])
        for vv in reads:
            for b in vv.bufs:
                if b.rd.get(tok[0], 0) < tok[1]:
                    b.rd[tok[0]] = tok[1]
        for vv in writes:
            for b in vv.bufs:
                b.lw = tok
                b.rd = {}
        self.ops[eng].append((waits, fn, tok))

    def mm(self, out, lhsT, rhs, start=True, stop=True, **kw):
        self.emit("pe", lambda e: e.matmul(out.ap, lhsT.ap, rhs.ap, start=start, stop=stop, **kw),
                  [lhsT, rhs], [out])

    def transpose(self, out, in_, ident):
        self.emit("pe", lambda e: e.transpose(out.ap, in_.ap, ident.ap), [in_, ident], [out])

    def act(self, out, in_, func, bias=None, scale=None, accum_out=None, eng="act"):
        reads = [in_]
        kw = {}
        if bias is not None:
            if isinstance(bias, V):
                reads.append(bias)
                kw["bias"] = bias.ap
            else:
                kw["bias"] = bias
        if scale is not None:
            if isinstance(scale, V):
                reads.append(scale)
                kw["scale"] = scale.ap
            else:
                kw["scale"] = scale
        writes = [out]
        if accum_out is not None:
            writes.append(accum_out)
            kw["accum_out"] = accum_out.ap
        self.emit("act", lambda e: e.activation(out.ap, in_.ap, func, **kw), reads, writes)

    def tt(self, eng, out, in0, in1, op):
        self.emit(eng, lambda e: e.tensor_tensor(out.ap, in0.ap, in1.ap, op), [in0, in1], [out])

    def ts(self, eng, out, in0, s1, op0, s2=None, op1=None, accum_out=None):
        reads = [in0]
        a1 = s1
        if isinstance(s1, V):
            reads.append(s1)
            a1 = s1.ap
        a2 = s2
        if isinstance(s2, V):
            reads.append(s2)
            a2 = s2.ap
        writes = [out]
        kw = {}
        if op1 is not None:
            kw["op1"] = op1
        if accum_out is not None:
            writes.append(accum_out)
            kw["accum_out"] = accum_out.ap
        self.emit(eng, lambda e: e.tensor_scalar(out.ap, in0.ap, a1, a2, op0, **kw), reads, writes)

    def stt(self, eng, out, in0, scalar, in1, op0, op1):
        reads = [in0, in1]
        sc = scalar
        if isinstance(scalar, V):
            reads.append(scalar)
            sc = scalar.ap
        self.emit(eng, lambda e: e.scalar_tensor_tensor(out.ap, in0.ap, sc, in1.ap, op0, op1), reads, [out])

    def copy(self, eng, out, in_):
        if eng == "act":
            self.emit("act", lambda e: e.copy(out.ap, in_.ap), [in_], [out])
        else:
            self.emit(eng, lambda e: e.tensor_copy(out.ap, in_.ap), [in_], [out])

    def memset(self, eng, out, val):
        self.emit(eng, lambda e: e.memset(out.ap, val), [], [out])

    def scan(self, eng, out, d0, d1, init, op0=ALU.mult, op1=ALU.add):
        reads = [d0, d1]
        ini = init
        if isinstance(init, V):
            reads.append(init)
            ini = init.ap
        self.emit(eng, lambda e: e.tensor_tensor_scan(out.ap, d0.ap, d1.ap, ini, op0, op1), reads, [out])

    def reduce(self, eng, out, in_, op=ALU.add, axis=AX.X):
        self.emit(eng, lambda e: e.tensor_reduce(out.ap, in_.ap, axis, op), [in_], [out])

    def recip(self, out, in_):
        self.emit("dve", lambda e: e.reciprocal(out.ap, in_.ap), [in_], [out])

    def dma(self, q, out, in_, sbuf_side=None, **kw):
        side = out if sbuf_side is None else sbuf_side
        b = side.bufs[0]
        if b.sem is None:
            b.sem = self.get_dsem()
        ds = b.sem
        sem = ds.sem
        self.emit(q, lambda e: e.dma_start(out.ap, in_.ap, **kw).then_inc(sem, 16), [in_], [out], dma_buf=ds)

    def finish(self, final_waits=()):
        nc = self.nc
        waited = {e: set() for e in ENGS}
        for e in ENGS:
            for waits, fn, tok in self.ops[e]:
                for k, v in waits:
                    if isinstance(k, str):
                        waited[k].add(v)
        for k, v in final_waits:
            if isinstance(k, str):
                waited[k].add(v)
        semmap = {}
        for e in ENGS:
            vals = sorted(waited[e])
            per = {}
            for v in vals:
                ep = v // EPOCH
                per.setdefault(ep, []).append(v)
            for ep, vs in per.items():
                sem = nc.alloc_semaphore("e_%s_%d" % (e, ep))
                self.nsem += 1
                for r, v in enumerate(vs):
                    semmap[(e, v)] = (sem, r + 1)
        engobj = {"pe": "tensor", "act": "scalar", "dve": "vector", "pool": "gpsimd", "sp": "sync"}

        def run(ename):
            def body(eng):
                for waits, fn, tok in self.ops[ename]:
                    for k, v in waits:
                        if isinstance(k, str):
                            sem, val = semmap[(k, v)]
                        else:
                            sem, val = k.sem, v
                        eng.wait_ge(sem, val)
                    ins = fn(eng)
                    if isinstance(tok[0], str):
                        sv = semmap.get(tok)
                        if sv is not None:
                            ins.then_inc(sv[0], 1)
                if ename == "sp":
                    for k, v in final_waits:
                        if isinstance(k, str):
                            sem, val = semmap[(k, v)]
                        else:
                            sem, val = k.sem, 16 * k.cnt
                        eng.wait_ge(sem, val)
            return body

        with nc.Block() as block:
            block.tensor(run("pe"))
            block.scalar(run("act"))
            block.vector(run("dve"))
            block.gpsimd(run("pool"))
            block.sync(run("sp"))
        return sum(len(self.ops[e]) for e in ENGS)

from concourse.bass_utils import run_bass_kernel_spmd

NT, CT, LT = 2304, 256, 2048
NTILE = 18
TBLK = [(0, 256)] + [(256 + 512 * i, 512) for i in range(4)]
EPS = 1e-6
SLABW = 512


class Rot:
    def __init__(self, items):
        self.items = items
        self.i = 0

    def next(self):
        t = self.items[self.i % len(self.items)]
        self.i += 1
        return t


def build(n_layers=2, dbg=None):
    nc = bass.Bass("TRN2", target_bir_lowering=False)
    P = Prog(nc)
    dbg = dbg or {}

    def din(name, shape, dt=F32):
        return P.dram(name, shape, dt, kind="ExternalInput")

    x_d = din("x", [LT, 1024])
    ctx_d = din("ctx", [CT, 1024])
    cc_d = din("cc", [128, 16])
    adaw_d = din("ada_w", [2, 1024, 6144])
    adab_d = din("adabT", [2, 128, 48])
    g1_d = din("g1T", [2, 128, 8])
    g2_d = din("g2T", [2, 128, 8])
    win_d = din("w_in", [2, 1024, 8192])
    convw_d = din("convT", [2, 128, 16])
    convb_d = din("convb", [2, 128, 4])
    bda_d = din("bda", [2, 2, 4, 128, 128])
    bdx_d = din("bdx", [2, 2, 4, 128, 128])
    lruv_d = din("lruv", [2, 128, 24])
    lbl_d = din("lbl", [128, 16])
    hgg_d = din("hgg", [2, 128])
    nag_d = din("nag", [2, 128, 2])
    nab_d = din("nab", [2, 8, 21, 128, 128])
    wbr_d = din("w_branch", [2, 3, 512, 1024])
    wout_d = din("w_out", [2, 1024, 1024])
    w1_d = din("ffn_w1", [2, 1024, 4096])
    w2_d = din("ffn_w2", [2, 4096, 1024])
    ident_d = din("ident", [128, 128])
    rope_d = din("rope_cs", [2, 128, 2048])
    rotT_d = din("rotT", [128, 128])
    blk_d = din("blk", [128, 128])
    rmask_d = din("rmask", [128, 2305])
    rmask64_d = din("rmask64", [128, 2305])
    tri_d = din("tri", [2, 64, 64])
    out_d = P.dram("out", [LT, 1024], F32, kind="ExternalOutput")
    hbuf = P.dram("hbuf", [NT, 1024], F32, kind=("ExternalOutput" if dbg.get("hbuf") else "Internal"))
    modD = P.dram("modD", [2, 2, 48, 128], F32)
    dbg_out = {}

    def dbg_tensor(name, shape, dt=F32):
        dbg_out[name] = P.dram(name, shape, dt, kind="ExternalOutput")
        return dbg_out[name]

    BIGCOLS = 53200
    big = nc.alloc_sbuf_tensor("big", [128, BIGCOLS], F32)
    state = {"off": 0}

    def take(name, shape, dtype=F32):
        n = 1
        for s in shape[1:]:
            n *= s
        ncols = n if dtype == F32 else (n + 1) // 2
        off = state["off"]
        assert off + ncols <= BIGCOLS, (name, off, ncols)
        ap = big[: shape[0], off: off + ncols]
        if dtype != F32:
            ap = ap.bitcast(dtype)
            if n % 2:
                ap = ap[:, :n]
        if len(shape) == 3:
            ap = ap.rearrange("p (a b) -> p a b", a=shape[1], b=shape[2])
        elif len(shape) == 4:
            ap = ap.rearrange("p (a b c) -> p a b c", a=shape[1], b=shape[2], c=shape[3])
        state["off"] = off + ncols
        return Tn(ap, name)

    psh = nc.alloc_psum_tensor("PS", [128, 8, 512], F32)
    class PTn(Tn):
        def part(self, key, idx):
            return V(self.h[idx], (self.whole,))

    PSB = [PTn(psh[:, b, :], "ps%d" % b) for b in range(8)]
    for t_ in PSB:
        t_.whole.excl = True

    ident = take("ident", [128, 128])
    identb = take("identb", [128, 128], BF16)
    tri = take("tri", [64, 2, 64])
    cct = take("cct", [128, 16])
    sct = take("sct", [128, 8, 2])
    M0 = take("M0", [128, 48])
    M1 = take("M1", [128, 48])
    AB = take("AB", [128, 2, 2, 8])
    g1t = take("g1t", [128, 8])
    g2t = take("g2t", [128, 8])
    small = take("small", [128, 64])
    lbt = take("lbt", [128, 16])
    lbv = take("lbv", [128, 2, 2, 4])
    lruv = take("lruv", [128, 24])
    cneg = take("cneg", [128, 8])
    convw = take("convw", [128, 16])
    convb = take("convb", [128, 4])
    nag = take("nag", [128, 2])
    uT = take("uT", [128, 8, NT], BF16)
    yT = take("yT", [128, 12, NT], BF16)
    PERS = state["off"]

    P.dma("sp", ident.v(), ident_d.v())
    P.dma("sp", tri.v(), V(tri_d.h[:].rearrange("a s t -> s a t"), tri_d.v().bufs))
    P.dma("sp", cct.v(), cc_d.v())
    P.dma("sp", lbt.v(), lbl_d.v())
    P.copy("dve", identb.v(), ident.v())
    P.act(V(sct.h[:].rearrange("p a b -> p (a b)"), sct.v().bufs), cct.v(), AF.Silu)

    bank_rot = Rot([PSB[0], PSB[1], PSB[2], PSB[3]])
    q_rot = Rot(["sp", "sp"])

    def sub(tn, idx):
        return V(tn.h[idx], tn.v().bufs)

    def h_src(l, i):
        if l == 0:
            if i < 2:
                return V(ctx_d.h[i * 128:(i + 1) * 128, :], (ctx_d.whole,))
            return V(x_d.h[(i - 2) * 128:(i - 1) * 128, :], (x_d.whole,))
        return hbuf.part(i, (slice(i * 128, (i + 1) * 128), slice(None)))

    def load_w(dst_v, src_tn, rows0, nk, c0, w):
        src = src_tn[rows0: rows0 + nk * 128, c0: c0 + w].rearrange("(k p) w -> p k w", p=128)
        P.dma("pool", dst_v, V(src, ()))

    def modulation(l):
        mark = state["off"]
        wts = Rot([take("mw%d" % i, [128, 8, 256], BF16) for i in range(4)])
        sctb = take("sctb", [128, 8, 2], BF16)
        P.copy("dve", sctb.v(), sct.v())
        mps = PSB[4]
        for s in range(24):
            wt = wts.next()
            src = adaw_d.h[l, :, s * 256:(s + 1) * 256].rearrange("(k p) w -> p k w", p=128)
            P.dma("pool", wt.v(), V(src, ()))
            for jj in range(2):
                j = s * 2 + jj
                for kc in range(8):
                    P.mm(sub(mps, (slice(None), slice(j * 2, j * 2 + 2))),
                         sub(wt, (slice(None), kc, slice(jj * 128, (jj + 1) * 128))),
                         sub(sctb, (slice(None), kc, slice(None))), start=(kc == 0), stop=(kc == 7))
        adab = take("adab", [128, 48])
        P.dma("sp", adab.v(), V(adab_d.h[l], ()))
        P.dma("sp", g1t.v(), V(g1_d.h[l], ()))
        P.dma("sp", g2t.v(), V(g2_d.h[l], ()))
        mp3 = mps.h[:, 0:96].rearrange("p (j s) -> p j s", s=2)
        for s, Ms in enumerate((M0, M1)):
            P.tt("dve", Ms.v(), V(mp3[:, :, s], mps.v().bufs), adab.v(), ALU.add)
            P.stt("dve", sub(AB, (slice(None), 0, s, slice(None))), sub(Ms, (slice(None), slice(8, 16))), 1.0,
                  g1t.v(), ALU.add, ALU.mult)
            P.stt("dve", sub(AB, (slice(None), 1, s, slice(None))), sub(Ms, (slice(None), slice(32, 40))), 1.0,
                  g2t.v(), ALU.add, ALU.mult)
            tp = PSB[5]
            P.transpose(sub(tp, (slice(0, 48), slice(0, 128))), Ms.v(), ident.v())
            tsb = take("mT%d" % s, [48, 128])
            P.copy("act", tsb.v(), sub(tp, (slice(0, 48), slice(0, 128))))
            P.dma("sp", V(modD.h[l, s], (modD.whole,)), tsb.v(), sbuf_side=tsb.v())
        state["off"] = mark

    def gate_row(dst, l, s, sec):
        src = modD.h[l, s, sec * 8:(sec + 1) * 8, :].rearrange("(o a) b -> o (a b)", o=1).partition_broadcast(128)
        P.dma("sp", dst.v(), V(src[:, 0, :], (modD.whole,)))

    def norm_tile(ht, i, which, scr):
        s = 1 if i < 2 else 0
        junk, ss, hn = scr
        P.memset("dve", ss.v(), 0.0)
        P.act(junk.v(), ht.v(), AF.Square, accum_out=sub(ss, (slice(None), slice(0, 1))))
        P.ts("dve", sub(ss, (slice(None), slice(1, 2))), sub(ss, (slice(None), slice(0, 1))), 1.0 / 1024, ALU.mult, EPS, ALU.add)
        P.act(sub(ss, (slice(None), slice(2, 3))), sub(ss, (slice(None), slice(1, 2))), AF.Sqrt)
        P.recip(sub(ss, (slice(None), slice(3, 4))), sub(ss, (slice(None), slice(2, 3))))
        P.ts("dve", hn.v(), ht.v(), sub(ss, (slice(None), slice(3, 4))), ALU.mult)

    def norm_tile2(i, which, scr):
        s = 1 if i < 2 else 0
        junk, ss, hn = scr
        for half in range(2):
            pb = bank_rot.next()
            for c4 in range(4):
                c = half * 4 + c4
                P.transpose(sub(pb, (slice(None), slice(c4 * 128, (c4 + 1) * 128))),
                            sub(hn, (slice(None), slice(c * 128, (c + 1) * 128))), ident.v())
            for c4 in range(4):
                c = half * 4 + c4
                bsec = 0 if which == 0 else 24
                Ms = M1 if s else M0
                if c4 % 2 == 0:
                    P.act(sub(uT, (slice(None), c, slice(i * 128, (i + 1) * 128))),
                          sub(pb, (slice(None), slice(c4 * 128, (c4 + 1) * 128))), AF.Identity,
                          bias=sub(Ms, (slice(None), slice(bsec + c, bsec + c + 1))),
                          scale=sub(AB, (slice(None), which, s, slice(c, c + 1))))
                else:
                    P.ts("dve", sub(uT, (slice(None), c, slice(i * 128, (i + 1) * 128))),
                         sub(pb, (slice(None), slice(c4 * 128, (c4 + 1) * 128))),
                         sub(AB, (slice(None), which, s, slice(c, c + 1))), ALU.mult,
                         sub(Ms, (slice(None), slice(bsec + c, bsec + c + 1))), ALU.add)

    def norm1_phase(l):
        mark = state["off"]
        hts = Rot([take("ht%d" % i, [128, 1024]) for i in range(3)])
        scrs = Rot([(take("junk%d" % i, [128, 1024], BF16), take("ss%d" % i, [128, 4]), take("hn%d" % i, [128, 1024])) for i in range(3)])
        tiles = range(NTILE)
        prev = None
        for i in tiles:
            ht = hts.next()
            P.dma("sp", ht.v(), h_src(l, i))
            sc_ = scrs.next()
            norm_tile(ht, i, 0, sc_)
            if prev is not None:
                norm_tile2(prev[0], 0, prev[1])
            prev = (i, sc_)
        norm_tile2(prev[0], 0, prev[1])
        state["off"] = mark

    def proj_fm(wv, ncol0, blocks, consume, kn=8, rhs_of=None):
        for (t0, tl) in blocks:
            pb = bank_rot.next()
            for kc in range(kn):
                rhs = sub(uT, (slice(None), kc, slice(t0, t0 + tl))) if rhs_of is None else rhs_of(kc, t0, tl)
                P.mm(sub(pb, (slice(None), slice(0, tl))), wv[:, kc, ncol0:ncol0 + 128], rhs,
                     start=(kc == 0), stop=(kc == kn - 1))
            consume(sub(pb, (slice(None), slice(0, tl))), t0, tl)

    def mixer_A(l):
        mark = state["off"]
        W = [take("aW%d" % i, [128, 2320]) for i in range(6)]
        XB = take("aXB", [128, NT], BF16)
        slx = take("aslx", [128, 8, 512], BF16)
        slg = take("aslg", [128, 8, 512], BF16)
        bdw = Rot([take("abd%d" % i, [128, 128], BF16) for i in range(4)])
        load_w(slx.v(), win_d.h[l], 0, 8, 0, 512)
        load_w(slg.v(), win_d.h[l], 0, 8, 512, 512)
        P.dma("sp", lruv.v(), V(lruv_d.h[l], ()))
        P.dma("sp", convw.v(), V(convw_d.h[l], ()))
        P.dma("sp", convb.v(), V(convb_d.h[l], ()))
        lam = sub(lruv, (slice(None), slice(16, 24)))
        P.act(cneg.v(), lam, AF.Exp, scale=-1.0)
        P.ts("dve", cneg.v(), cneg.v(), 1.0, ALU.add)
        P.act(cneg.v(), cneg.v(), AF.Ln)
        P.ts("dve", cneg.v(), cneg.v(), -8.0, ALU.mult)
        W1, W2, W3, W4, W5, W6 = W
        W3b, W4b, W5b = [take("aWb%d" % i, [128, 2320]) for i in range(3)]
        segs = [(0, 256, 2), (256, 2304, 262 - 256)]
        for cc in range(4):
            for (a, b) in ((0, 2), (258, 262), (2310, 2312)):
                P.memset("pool", sub(W1, (slice(None), slice(a, b))), 0.0)

            def ev_x(pv, t0, tl):
                o = t0 + (2 if t0 < 256 else 6)
                P.copy("act", sub(W1, (slice(None), slice(o, o + tl))), pv)
            proj_fm(slx.v(), cc * 128, TBLK, ev_x)
            for (a, b, po) in segs:
                n = b - a
                o = sub(W2, (slice(None), slice(a, b)))

                def tap(j):
                    return sub(W1, (slice(None), slice(a + po + j - 2, a + po + j - 2 + n)))

                def cw(j):
                    return sub(convw, (slice(None), slice(cc * 4 + j, cc * 4 + j + 1)))
                P.ts("dve", o, tap(0), cw(0), ALU.mult, sub(convb, (slice(None), slice(cc, cc + 1))), ALU.add)
                for j in (1, 2, 3):
                    P.stt("dve", o, tap(j), cw(j), o, ALU.mult, ALU.add)
            P.copy("act", XB.v(), sub(W2, (slice(None), slice(0, NT))))
            full = (slice(None), slice(0, NT))

            def rv(tn, a_, b_):
                return V(tn.h[:, a_:b_][:, ::-1], tn.v().bufs)

            def dir_steps(d):
                R3, R4, R5 = (W3, W4, W5) if d == 0 else (W3b, W4b, W5b)
                wa = bdw.next()
                wx = bdw.next()
                ba = sub(lruv, (slice(None), slice(0 + d * 4 + cc, 0 + d * 4 + cc + 1)))
                bx = sub(lruv, (slice(None), slice(8 + d * 4 + cc, 8 + d * 4 + cc + 1)))
                cn = sub(cneg, (slice(None), slice(d * 4 + cc, d * 4 + cc + 1)))

                def s0():
                    P.dma("pool", wa.v(), V(bda_d.h[l, d, cc], ()))
                    P.dma("pool", wx.v(), V(bdx_d.h[l, d, cc], ()))
                    for (t0, tl) in TBLK:
                        pb = bank_rot.next()
                        P.mm(sub(pb, (slice(None), slice(0, tl))), wa.v(), sub(XB, (slice(None), slice(t0, t0 + tl))))
                        P.act(sub(R3, (slice(None), slice(t0, t0 + tl))), sub(pb, (slice(None), slice(0, tl))), AF.Sigmoid, bias=ba)
                        pb = bank_rot.next()
                        P.mm(sub(pb, (slice(None), slice(0, tl))), wx.v(), sub(XB, (slice(None), slice(t0, t0 + tl))))
                        P.act(sub(R5, (slice(None), slice(t0, t0 + tl))), sub(pb, (slice(None), slice(0, tl))), AF.Sigmoid, bias=bx)

                def s7():
                    if d == 0:
                        P.scan("dve", sub(W6, full), sub(R3, full), sub(R5, full), 0.0)
                    else:
                        P.scan("dve", rv(W2, 0, 256), rv(R3, 0, 256), rv(R5, 0, 256), 0.0)
                        P.scan("dve", rv(W2, 256, NT), rv(R3, 256, NT), rv(R5, 256, NT), sub(W2, (slice(None), slice(0, 1))))
                return [
                    s0,
                    lambda: P.act(sub(R3, full), sub(R3, full), AF.Exp, scale=cn),
                    lambda: P.act(sub(R4, full), sub(R3, full), AF.Square),
                    lambda: P.ts("dve", sub(R4, full), sub(R4, full), -1.0, ALU.mult, 1.0, ALU.add),
                    lambda: P.act(sub(R4, full), sub(R4, full), AF.Sqrt),
                    lambda: P.tt("dve", sub(R5, full), sub(R5, full), sub(W2, full), ALU.mult),
                    lambda: P.tt("dve", sub(R5, full), sub(R5, full), sub(R4, full), ALU.mult),
                    s7,
                ]
            st_ = [dir_steps(0), dir_steps(1)]
            for k_ in range(8):
                for d in range(2):
                    st_[d][k_]()
            def ev_g(pv, t0, tl):
                P.copy("act", sub(W1, (slice(None), slice(t0, t0 + tl))), pv)
            proj_fm(slg.v(), cc * 128, TBLK, ev_g)
            full = (slice(None), slice(0, NT))
            P.act(sub(W4, full), sub(W1, full), AF.Square)
            P.ts("dve", sub(W4, full), sub(W4, full), 0.044715, ALU.mult, 1.0, ALU.add)
            P.tt("dve", sub(W4, full), sub(W4, full), sub(W1, full), ALU.mult)
            P.act(sub(W4, full), sub(W4, full), AF.Sigmoid, scale=1.5957691216057308)
            P.tt("dve", sub(W4, full), sub(W4, full), sub(W1, full), ALU.mult)
            P.tt("dve", sub(W6, full), sub(W6, full), sub(W2, full), ALU.add)
            P.tt("dve", sub(yT, (slice(None), cc, slice(None))), sub(W6, full), sub(W4, full), ALU.mult)
        state["off"] = mark

    def mixer_B(l):
        mark = state["off"]
        CH = 32 if l == 0 else 64
        NCH = NT // CH
        bank_rot.items = [PSB[0], PSB[1]]
        QS = take("bQS", [128, NT])
        V64 = take("bV64", [CH, NCH, 128], BF16)
        O64 = take("bO64", [CH, NCH, 128])
        RM = take("bRM", [128, 513])
        sl = take("bsl", [128, 8, 512], BF16)
        gN = take("bgN", [CH, 128])
        DEC = take("bDEC", [128, 2, NCH])
        Sst = [take("bS%d" % d, [128, 128]) for d in range(2)]
        SB3 = [[take("bSB%d%d" % (d, i), [128, 128], BF16) for i in range(3)] for d in range(2)]
        tmps = dict(T1=take("bT1", [128, 512]), KK=take("bKK", [128, 512]), B=take("bB", [128, 512]),
                    T2=take("bT2", [128, 512]), KHT=take("bKHT", [128, 512], BF16))
        tmps["T3"] = tmps["T1"]
        if CH == 64:
            tmps1 = dict(T1=take("bT1b", [128, 512]), KK=take("bKKb", [128, 512]), B=take("bBb", [128, 512]),
                         T2=take("bT2b", [128, 512]), KHT=take("bKHTb", [128, 512], BF16))
            tmps1["T3"] = tmps1["T1"]
        else:
            tmps1 = tmps
        tmpsD = [tmps, tmps1]
        AMd = [take("bAM%d" % d, [CH, 2, CH], BF16) for d in range(2)]

        def blkbufs(d, tag):
            d_ = dict(tmpsD[d])
            d_.update(QH=take("bQH" + tag, [128, 512], BF16), QT=take("bQT" + tag, [128, 512], BF16),
                      KT=take("bKT" + tag, [128, 512], BF16),
                      KH64=take("bKH64" + tag, [CH, 512 // CH, 128], BF16), AM=AMd[d])
            return d_
        BB = [Rot([blkbufs(d, "%d%d" % (d, i)) for i in range(2)]) for d in range(2)]
        fin = dict(ss=take("bfss", [CH, 4, NCH]))
        on1 = Tn(tmps["T2"].h[0:CH, 0:512].rearrange("p (a b) -> p a b", a=4), "bfon")
        on1.whole = tmps["T2"].whole
        fin["on"] = on1
        on2 = Tn(tmps["KK"].h[0:CH, 0:512].rearrange("p (a b) -> p a b", a=4), "bfon2")
        on2.whole = tmps["KK"].whole
        fin["on2"] = on2
        P.dma("sp", RM.v(), V((rmask_d if CH == 32 else rmask64_d).h[:, 0:513], ()))
        P.dma("sp", gN.v(), V(hgg_d.h[l:l + 1, :].partition_broadcast(CH)[:, 0, :], ()))
        lb4 = lbt.h[:].rearrange("p (d l h) -> p d l h", d=2, l=2, h=4)
        sm = small.h[:].rearrange("p (a d h) -> p a d h", a=8, d=2, h=4)
        sv = small.v().bufs
        P.tt("dve", V(sm[:, 0], sv), V(lb4[:, :, 0, :], lbt.v().bufs), V(lb4[:, :, 1, :], lbt.v().bufs), ALU.subtract)
        P.act(V(sm[:, 1], sv), V(sm[:, 0], sv), AF.Sigmoid)
        P.act(V(sm[:, 2], sv), V(sm[:, 0], sv), AF.Sigmoid, scale=-1.0)
        if l == 0:
            P.tt("dve", V(lbv.h[:, 0], lbv.v().bufs), V(sm[:, 1], sv), V(sm[:, 1], sv), ALU.subtract)
        else:
            P.tt("dve", V(sm[:, 3], sv), V(sm[:, 1], sv), V(sm[:, 2], sv), ALU.add)
            P.tt("dve", V(lbv.h[:, 0], lbv.v().bufs), V(sm[:, 3], sv), V(sm[:, 1], sv), ALU.subtract)
        P.ts("dve", V(lbv.h[:, 1], lbv.v().bufs), V(lbv.h[:, 0], lbv.v().bufs), -1.0, ALU.mult, 1.0, ALU.add)

        for h in range(4):
            slot = (slice(None), slice(None), slice(256, 384))
            slotB = (slice(None), slice(None), slice(384, 512))

            def ld_f(hh_):
                for j, c0 in enumerate((1536, 2048)):
                    load_w(sub(sl, (slice(None), slice(None), slice(j * 128, (j + 1) * 128))), win_d.h[l], 0, 8, c0 + hh_ * 128, 128)
            if h == 0:
                ld_f(0)
                load_w(sub(sl, slot), win_d.h[l], 0, 8, 1024, 128)
                load_w(sub(sl, slotB), win_d.h[l], 0, 8, 2560, 128)
            def ev_q(pv, t0, tl):
                P.act(sub(QS, (slice(None), slice(t0, t0 + tl))), pv, AF.Silu)
            proj_fm(sl.v(), 256, TBLK, ev_q)
            load_w(sub(sl, slot), win_d.h[l], 0, 8, 3072 + h * 128, 128)
            P.memset("pool", V(O64.h[:], tuple(O64.part(g_, (slice(None), g_, slice(None))).bufs[0] for g_ in range(NCH))), 0.0)
            for d in range(2):
                P.memset("pool", Sst[d].v(), 0.0)
                P.memset("pool", SB3[d][0].v(), 0.0)
            blocks = {0: TBLK, 1: [TBLK[0]] + TBLK[:0:-1]}
            stepc = [0, 0]
            def prep(d, t0, tl, bb):
                nch = tl // CH
                T1, KK, B_, T2, T3 = bb["T1"], bb["KK"], bb["B"], bb["T2"], bb["T3"]
                sl_ = (slice(None), slice(0, tl))
                lbs = V(lbv.h[:, 0, d, h:h + 1], lbv.v().bufs)
                oml = V(lbv.h[:, 1, d, h:h + 1], lbv.v().bufs)

                def ev_f(pv, t0_, tl_):
                    P.act(sub(T1, sl_), pv, AF.Sigmoid)
                proj_fm(sl.v(), 128 * d, [(t0, tl)], ev_f)
                yield
                P.ts("dve", sub(T1, sl_), sub(T1, sl_), oml, ALU.mult, lbs, ALU.add)
                yield
                P.ts("pool", sub(KK, sl_), sub(T1, sl_), -1.0, ALU.mult, 1.0, ALU.add)
                yield
                P.act(sub(T1, sl_), sub(T1, sl_), AF.Ln)
                yield
                if d == 0:
                    P.scan("dve", sub(B_, sl_), sub(RM, (slice(None), slice(0, tl))), sub(T1, sl_), 0.0)
                    yield
                    ilast, imid = CH - 1, CH // 2 - 1
                else:
                    P.scan("dve", V(B_.h[:, 0:tl][:, ::-1], B_.v().bufs),
                           V(RM.h[:, 1:tl + 1][:, ::-1], RM.v().bufs),
                           V(T1.h[:, 0:tl][:, ::-1], T1.v().bufs), 0.0)
                    yield
                    ilast, imid = 0, CH // 2
                B3 = B_.h[:, 0:tl].rearrange("p (c t) -> p c t", t=CH)
                bbf = B_.v().bufs

                def v3(tn):
                    return V(tn.h[:, 0:tl].rearrange("p (c t) -> p c t", t=CH), tn.v().bufs)
                P.act(sub(T2, sl_), sub(B_, sl_), AF.Exp)
                yield
                c0 = t0 // CH
                P.copy("pool", V(DEC.h[:, d, c0:c0 + nch], DEC.v().bufs),
                       V(T2.h[:, 0:tl].rearrange("p (c t) -> p c t", t=CH)[:, :, ilast], T2.v().bufs))
                yield
                P.tt("pool", sub(bb["QH"], sl_), sub(QS, (slice(None), slice(t0, t0 + tl))), sub(T2, sl_), ALU.mult)
                yield
                P.tt("dve", v3(T2), V(B3, bbf), V(B3[:, :, imid:imid + 1].broadcast_to([128, nch, CH]), bbf), ALU.subtract)
                yield
                P.act(sub(T3, sl_), sub(T2, sl_), AF.Exp)
                yield
                P.tt("pool", sub(bb["QT"], sl_), sub(QS, (slice(None), slice(t0, t0 + tl))), sub(T3, sl_), ALU.mult)
                yield
                P.act(sub(T3, sl_), sub(T2, sl_), AF.Exp, scale=-1.0)
                yield
                P.tt("pool", sub(bb["KT"], sl_), sub(KK, sl_), sub(T3, sl_), ALU.mult)
                yield
                P.tt("dve", v3(T2), V(B3[:, :, ilast:ilast + 1].broadcast_to([128, nch, CH]), bbf), V(B3, bbf), ALU.subtract)
                yield
                P.act(sub(T3, sl_), sub(T2, sl_), AF.Exp)
                yield
                P.tt("pool", sub(bb["KHT"], sl_), sub(KK, sl_), sub(T3, sl_), ALU.mult)
                yield
                for _ in range(18 if CH == 64 else 0):
                    yield
                for g4 in range(nch // 4):
                    tp = bank_rot.next()
                    tpb = V(tp.h[0:CH, :].bitcast(BF16), tp.v().bufs)
                    for q4 in range(4):
                        n = g4 * 4 + q4
                        P.transpose(tpb[:, q4 * 128:(q4 + 1) * 128], sub(bb["KHT"], (slice(None), slice(n * CH, (n + 1) * CH))), identb.v())
                        yield
                    P.copy("act", V(bb["KH64"].h[:, g4 * 4:(g4 + 1) * 4, :].rearrange("p a b -> p (a b)"), bb["KH64"].v().bufs),
                           tpb[:, 0:512])
                    yield


            import itertools
            binfo0 = [None, None]
            g0s = []
            for d in range(2):
                t0_, tl_ = blocks[d][0]
                bb_ = BB[d].next()
                g0s.append(prep(d, t0_, tl_, bb_))
                binfo0[d] = (bb_, t0_, tl_, tl_ // CH, t0_ // CH)
            gen0 = itertools.chain(*g0s)
            def ev_v(pv, t0, tl):
                P.copy("act", sub(yT, (slice(None), 4 + h, slice(t0, t0 + tl))), pv)
            proj_fm(sl.v(), 384, TBLK, ev_v)
            for g4 in range(NCH // 4):
                tp = bank_rot.next()
                tpb = V(tp.h[0:CH, :].bitcast(BF16), tp.v().bufs)
                for q4 in range(4):
                    n = g4 * 4 + q4
                    P.transpose(tpb[:, q4 * 128:(q4 + 1) * 128], sub(yT, (slice(None), 4 + h, slice(n * CH, (n + 1) * CH))), identb.v())
                P.copy("act", V(V64.h[:, g4 * 4:(g4 + 1) * 4, :].rearrange("p a b -> p (a b)"), V64.v().bufs), tpb[:, 0:512])
                for _ in range(5):
                    next(gen0, None)
            if h < 3:
                load_w(sub(sl, slotB), win_d.h[l], 0, 8, 2560 + (h + 1) * 128, 128)
            for _ in gen0:
                pass
            pending_gen = None
            nxt_info = None
            for bi in range(5):
                if bi == 0:
                    binfo = binfo0
                else:
                    for _ in pending_gen:
                        pass
                    binfo = nxt_info
                if bi < 4:
                    nxt_info = [None, None]
                    gens = []
                    for d in range(2):
                        t0, tl = blocks[d][bi + 1]
                        bb = BB[d].next()
                        gens.append(prep(d, t0, tl, bb))
                        nxt_info[d] = (bb, t0, tl, tl // CH, t0 // CH)
                    if CH == 64:
                        def _rr(a_, b_):
                            for xa, xb in itertools.zip_longest(a_, b_):
                                yield
                        pending_gen = _rr(gens[0], gens[1])
                    else:
                        pending_gen = itertools.chain(*gens)
                else:
                    pending_gen = iter(())
                pend = [None, None]

                def o_part(d, item):
                    bb_, n_, gn_, par_, sbi_ = item
                    cs_ = slice(n_ * CH, (n_ + 1) * CH)
                    pO = sub(PSB[(7, 2)[par_]], (slice(0, CH), slice(d * 128, (d + 1) * 128)))
                    am_ = bb_["AM"].part(par_, (slice(None), par_, slice(None)))
                    vch_ = sub(V64, (slice(None), gn_, slice(None)))
                    P.mm(pO, am_, vch_, start=True, stop=False)
                    P.mm(pO, sub(bb_["QH"], (slice(None), cs_)), SB3[d][sbi_].v(), start=False, stop=True)
                    och = O64.part(gn_, (slice(None), gn_, slice(None)))
                    P.tt("dve", och, och, pO, ALU.add)

                nsteps = binfo[0][3]
                for st in range(nsteps):
                    for d in range(2):
                        bb, t0, tl, nch, c0 = binfo[d]
                        n = st if d == 0 else nch - 1 - st
                        gn = c0 + n
                        k = stepc[d]
                        stepc[d] += 1
                        par = (k + d) % 2
                        cs = slice(n * CH, (n + 1) * CH)
                        pA = sub(PSB[3 + par], (slice(0, CH), slice(d * CH, (d + 1) * CH)))
                        pS = sub(PSB[5 + par], (slice(None), slice(d * 128, (d + 1) * 128)))
                        P.mm(pA, sub(bb["KT"], (slice(None), cs)), sub(bb["QT"], (slice(None), cs)))
                        am = bb["AM"].part(par, (slice(None), par, slice(None)))
                        P.tt("dve", am, pA, sub(tri, (slice(0, CH), d, slice(0, CH))), ALU.mult)
                        vch = sub(V64, (slice(None), gn, slice(None)))
                        P.mm(pS, sub(bb["KH64"], (slice(None), n, slice(None))), vch)
                        if pend[d] is not None:
                            o_part(d, pend[d])
                        pend[d] = (bb, n, gn, par, k % 3)
                        P.stt("dve", Sst[d].v(), Sst[d].v(), V(DEC.h[:, d, gn:gn + 1], DEC.v().bufs), pS, ALU.mult, ALU.add)
                        P.copy("act", SB3[d][(k + 1) % 3].v(), Sst[d].v())
                    for _ in range(6 if CH == 32 else 8):
                        next(pending_gen, None)
                for d in range(2):
                    if pend[d] is not None:
                        o_part(d, pend[d])
                        pend[d] = None
            if h < 3:
                ld_f(h + 1)

            def ev_og(pv, t0, tl):
                P.act(sub(QS, (slice(None), slice(t0, t0 + tl))), pv, AF.Sigmoid)
            proj_fm(sl.v(), 256, TBLK, ev_og)
            if h < 3:
                load_w(sub(sl, slot), win_d.h[l], 0, 8, 1024 + (h + 1) * 128, 128)
            ss = fin["ss"]
            for g4 in range(NCH // 4):
                o4 = O64.parts_v([g4 * 4 + i_ for i_ in range(4)], (slice(None), slice(g4 * 4, g4 * 4 + 4), slice(None)))
                sqb = fin["on"] if g4 % 2 == 0 else fin["on2"]
                P.act(sqb.v(), o4, AF.Square)
                P.reduce("dve", sub(ss, (slice(None), 0, slice(g4 * 4, g4 * 4 + 4))), sqb.v())
            P.ts("dve", sub(ss, (slice(None), 1, slice(None))), sub(ss, (slice(None), 0, slice(None))), 1.0 / 128, ALU.mult, EPS, ALU.add)
            P.act(sub(ss, (slice(None), 2, slice(None))), sub(ss, (slice(None), 1, slice(None))), AF.Sqrt)
            P.recip(sub(ss, (slice(None), 3, slice(None))), sub(ss, (slice(None), 2, slice(None))))
            for g4 in range(NCH // 4):
                o4 = O64.parts_v([g4 * 4 + i_ for i_ in range(4)], (slice(None), slice(g4 * 4, g4 * 4 + 4), slice(None)))
                onb = fin["on"] if g4 % 2 == 0 else fin["on2"]
                P.tt("dve", onb.v(), o4, V(ss.h[:, 3, g4 * 4:g4 * 4 + 4].unsqueeze(2).broadcast_to([CH, 4, 128]), ss.v().bufs), ALU.mult)
                P.tt("pool", onb.v(), onb.v(), V(gN.h[:].unsqueeze(1).broadcast_to([CH, 4, 128]), gN.v().bufs), ALU.mult)
                tp = bank_rot.next()
                for q4 in range(4):
                    P.transpose(sub(tp, (slice(None), slice(q4 * CH, (q4 + 1) * CH))), sub(onb, (slice(None), q4, slice(None))),
                                sub(ident, (slice(0, CH), slice(0, CH))))
                tsl = slice(g4 * 4 * CH, (g4 + 1) * 4 * CH)
                P.tt("dve", sub(yT, (slice(None), 4 + h, tsl)), sub(tp, (slice(None), slice(0, 4 * CH))), sub(QS, (slice(None), tsl)), ALU.mult)
        bank_rot.items = [PSB[0], PSB[1], PSB[2], PSB[3]]
        state["off"] = mark

    def pair_tiles(p):
        r = 2 * p
        if r in (0, 2):
            return [(j, 5 + (r // 2) * 4 + j) for j in range(4)]
        if r in (28, 30):
            return [(12 + j, 13 + ((r - 28) // 2) * 4 + j) for j in range(4)]
        return [((r - 4) // 2 + j, j) for j in range(5)]

    def mixer_C(l, do_ctx):
        mark = state["off"]
        blk = take("blk", [128, 128])
        rotT = take("rotT", [128, 128])
        P.dma("sp", blk.v(), blk_d.v())
        P.dma("sp", rotT.v(), rotT_d.v())
        blkb = take("blkb", [128, 128], BF16)
        P.copy("dve", blkb.v(), blk.v())
        slq = take("cslq", [128, 8, 512], BF16)
        slk = take("cslk", [128, 8, 512], BF16)
        slv = take("cslv", [128, 8, 512], BF16)
        KTr = take("cKTr", [128, NT], BF16)
        QTr = take("cQTr", [128, NT], BF16)
        QTp = take("cQTp", [128, NT], BF16)
        VA = take("cVA", [128, 18, 2, 65], BF16)
        NABf = Rot([take("cNABf%d" % i, [128, 7, 128]) for i in range(2)])
        NAB = take("cNAB", [128, 2, 21, 128], BF16)
        def mk_tb(i):
            d_ = dict(X=take("cX%d" % i, [128, 512]), SQ=take("cSQ%d" % i, [128, 512]), RS=take("cRS%d" % i, [128, 512]),
                      XN=take("cXN%d" % i, [128, 512]), CS=take("cCS%d" % i, [128, 2, 512]), SQb=take("cSQb%d" % i, [128, 512], BF16))
            d_["T"] = d_["SQ"]
            d_["U"] = d_["RS"]
            return d_
        tb = Rot([mk_tb(i) for i in range(3)])
        PT = Rot([take("cPT%d" % i, [128, 7, 128], BF16) for i in range(4)])
        scsets = Rot([(PSB[6], PSB[7]), (PSB[0], PSB[1]), (PSB[2], PSB[3])])
        PTc = take("cPTc", [128, 2, 256], BF16)
        fin = Rot([dict(rec=take("cfr%d" % i, [128, 2]), y=take("cfy%d" % i, [128, 2, 64])) for i in range(2)])
        load_w(slq.v(), win_d.h[l], 0, 8, 3584, 512)
        load_w(slk.v(), win_d.h[l], 0, 8, 4096, 512)
        load_w(slv.v(), win_d.h[l], 0, 8, 4608, 512)
        P.dma("sp", nag.v(), V(nag_d.h[l], ()))
        P.memset("pool", V(VA.h[:, :, :, 64:65], VA.v().bufs), 1.0)
        for c in range(4):
            for i in range(NTILE):
                pb = bank_rot.next()
                for kc in range(8):
                    P.mm(sub(pb, (slice(None), slice(0, 128))), sub(uT, (slice(None), kc, slice(i * 128, (i + 1) * 128))),
                         sub(slv, (slice(None), kc, slice(c * 128, (c + 1) * 128))), start=(kc == 0), stop=(kc == 7))
                P.copy("act", V(VA.h[:, i, :, 0:64], VA.v().bufs),
                       V(pb.h[:, 0:128].rearrange("p (a b) -> p a b", a=2), pb.v().bufs))
            items = [(which, slw, t0, tl) for which, slw in ((0, slq), (1, slk)) for (t0, tl) in TBLK]
            live = {}

            def st1(i):
                which, slw, t0, tl = items[i]
                t = tb.next()
                live[i] = t
                sl_ = (slice(None), slice(0, tl))

                def ev(pv, t0_, tl_):
                    P.copy("act", sub(t["X"], sl_), pv)
                proj_fm(slw.v(), c * 128, [(t0, tl)], ev)
                P.act(sub(t["SQb"], sl_), sub(t["X"], sl_), AF.Square)
                if t0 >= 256:
                    P.dma("sp", t["CS"].v(), V(rope_d.h[:, :, t0 - 256:t0 - 256 + 512].rearrange("a p t -> p a t"), ()))

            def st2(i):
                which, slw, t0, tl = items[i]
                t = live[i]
                sl_ = (slice(None), slice(0, tl))
                gsc = sub(nag, (slice(None), slice(which, which + 1)))
                pb = bank_rot.next()
                P.mm(sub(pb, sl_), blkb.v(), sub(t["SQb"], sl_))
                P.ts("dve", sub(t["RS"], sl_), sub(pb, sl_), EPS, ALU.add)
                P.act(sub(t["RS"], sl_), sub(t["RS"], sl_), AF.Sqrt)
                P.recip(sub(t["RS"], sl_), sub(t["RS"], sl_))
                P.stt("dve", sub(t["XN"], sl_), sub(t["X"], sl_), gsc, sub(t["RS"], sl_), ALU.mult, ALU.mult)
                dstp = QTp if which == 0 else KTr
                if which == 0 or t0 < 256:
                    P.copy("act", sub(dstp, (slice(None), slice(t0, t0 + tl))), sub(t["XN"], sl_))

            def st3(i):
                which, slw, t0, tl = items[i]
                t = live.pop(i)
                sl_ = (slice(None), slice(0, tl))
                if t0 >= 256:
                    pb2 = bank_rot.next()
                    P.mm(sub(pb2, sl_), rotT.v(), sub(t["XN"], sl_))
                    P.tt("dve", sub(t["T"], sl_), sub(pb2, sl_), sub(t["CS"], (slice(None), 1, slice(None))), ALU.mult)
                    P.tt("pool", sub(t["U"], sl_), sub(t["XN"], sl_), sub(t["CS"], (slice(None), 0, slice(None))), ALU.mult)
                    dstr = QTr if which == 0 else KTr
                    P.tt("dve", sub(dstr, (slice(None), slice(t0, t0 + tl))), sub(t["T"], sl_), sub(t["U"], sl_), ALU.add)
            ni = len(items)
            for i in range(ni + 2):
                if i < ni:
                    st1(i)
                if 0 <= i - 1 < ni:
                    st2(i - 1)
                if 0 <= i - 2 < ni:
                    st3(i - 2)
            for hh in range(2):
                for g3 in range(3):
                    nf = NABf.next()
                    P.dma("sp", nf.v(), V(nab_d.h[l, 2 * c + hh, g3 * 7:(g3 + 1) * 7].rearrange("m k q -> k m q"), ()))
                    P.act(sub(NAB, (slice(None), hh, slice(g3 * 7, (g3 + 1) * 7), slice(None))), nf.v(), AF.Exp)
            aitems = [(p, hh) for p in range(16) for hh in range(2)]
            alive = {}
            fins = {}

            def obank(p):
                return PSB[5] if p % 2 == 0 else PSB[4]

            def a_scores(i):
                p, hh = aitems[i]
                tiles = pair_tiles(p)
                q0 = 256 + p * 128
                hs = slice(hh * 64, (hh + 1) * 64)
                scA, scB = scsets.next()

                def scv(j):
                    bk = scA if j < 4 else scB
                    jj = j % 4
                    return sub(bk, (slice(None), slice(jj * 128, (jj + 1) * 128)))
                nl = len(tiles)
                for j, (kt, mi) in enumerate(tiles):
                    k0 = 256 + kt * 128
                    P.mm(scv(j), sub(KTr, (hs, slice(k0, k0 + 128))), sub(QTr, (hs, slice(q0, q0 + 128))))
                for j in range(2):
                    P.mm(scv(nl + j), sub(KTr, (hs, slice(j * 128, (j + 1) * 128))), sub(QTp, (hs, slice(q0, q0 + 128))))
                pt = PT.next()
                n1 = min(4, nl + 2)
                P.act(V(pt.h[:, 0:n1, :].rearrange("p a b -> p (a b)"), pt.v().bufs), sub(scA, (slice(None), slice(0, n1 * 128))), AF.Exp, scale=0.125)
                n2 = nl + 2 - n1
                if n2 > 0:
                    P.act(V(pt.h[:, 4:4 + n2, :].rearrange("p a b -> p (a b)"), pt.v().bufs), sub(scB, (slice(None), slice(0, n2 * 128))), AF.Exp, scale=0.125)
                mi0 = tiles[0][1]
                P.tt("pool", sub(pt, (slice(None), slice(0, nl), slice(None))), sub(pt, (slice(None), slice(0, nl), slice(None))),
                     sub(NAB, (slice(None), hh, slice(mi0, mi0 + nl), slice(None))), ALU.mult)
                alive[i] = (pt, tiles, nl)

            def a_pv(i):
                p, hh = aitems[i]
                pt, tiles, nl = alive.pop(i)
                oacc_v = sub(obank(p), (slice(None), slice(0, 130)))
                ov = oacc_v[:, hh * 65:hh * 65 + 65]
                for j in range(nl + 2):
                    vt = (2 + tiles[j][0]) if j < nl else (j - nl)
                    P.mm(ov, sub(pt, (slice(None), j, slice(None))), sub(VA, (slice(None), vt, hh, slice(None))),
                         start=(j == 0), stop=(j == nl + 1))
                if hh == 1:
                    f = fin.next()
                    fins[p] = f
                    o3 = oacc_v.ap.rearrange("p (a b) -> p a b", a=2)
                    P.recip(f["rec"].v(), V(o3[:, :, 64], oacc_v.bufs))
                    P.tt("dve", f["y"].v(), V(o3[:, :, 0:64], oacc_v.bufs),
                         V(f["rec"].h[:].unsqueeze(2).broadcast_to([128, 2, 64]), f["rec"].v().bufs), ALU.mult)

            def a_fin(p):
                f = fins.pop(p)
                q0 = 256 + p * 128
                tpv = sub(obank(p), (slice(None), slice(256, 384)))
                P.transpose(tpv, V(f["y"].h[:].rearrange("p a b -> p (a b)"), f["y"].v().bufs), ident.v())
                P.copy("act", sub(yT, (slice(None), 8 + c, slice(q0, q0 + 128))), tpv)
            na = len(aitems)
            a_scores(0)
            a_scores(1)
            for i in range(na):
                if i + 2 < na:
                    a_scores(i + 2)
                a_pv(i)
                p, hh = aitems[i]
                if hh == 1 and p >= 1:
                    a_fin(p - 1)
            a_fin(15)
            if do_ctx:
                for hh in range(2):
                    hs = slice(hh * 64, (hh + 1) * 64)
                    scA = PSB[6]
                    for j in range(2):
                        P.mm(sub(scA, (slice(None), slice(j * 256, (j + 1) * 256))), sub(KTr, (hs, slice(j * 128, (j + 1) * 128))),
                             sub(QTp, (hs, slice(0, 256))))
                    P.act(V(PTc.h[:].rearrange("p a b -> p (a b)"), PTc.v().bufs), sub(scA, (slice(None), slice(0, 512))), AF.Exp, scale=0.125)
                    for i in range(2):
                        oacc = PSB[5]
                        ov = sub(oacc, (slice(None), slice((i * 2 + hh) * 65, (i * 2 + hh) * 65 + 65)))
                        for j in range(2):
                            P.mm(ov, sub(PTc, (slice(None), j, slice(i * 128, (i + 1) * 128))), sub(VA, (slice(None), j, hh, slice(None))),
                                 start=(j == 0), stop=(j == 1))
                for i in range(2):
                    f = fin.next()
                    oacc = PSB[5]
                    o3 = oacc.h[:, i * 130:(i + 1) * 130].rearrange("p (a b) -> p a b", a=2)
                    P.recip(f["rec"].v(), V(o3[:, :, 64], oacc.v().bufs))
                    P.tt("dve", f["y"].v(), V(o3[:, :, 0:64], oacc.v().bufs),
                         V(f["rec"].h[:].unsqueeze(2).broadcast_to([128, 2, 64]), f["rec"].v().bufs), ALU.mult)
                    tp = PSB[4]
                    P.transpose(sub(tp, (slice(None), slice(0, 128))), V(f["y"].h[:].rearrange("p a b -> p (a b)"), f["y"].v().bufs), ident.v())
                    P.copy("act", sub(yT, (slice(None), 8 + c, slice(i * 128, (i + 1) * 128))), sub(tp, (slice(None), slice(0, 128))))
        state["off"] = mark

    def merge_phase(l, blocks, tiles):
        mark = state["off"]
        MG = take("mMG", [128, 8, NT], BF16)
        wg = Rot([take("mwg%d" % i, [128, 8, 384], BF16) for i in range(2)])
        wb = Rot([take("mwb%d" % i, [128, 12, 128], BF16) for i in range(2)])
        sg = Rot([take("msg%d" % i, [128, 512]) for i in range(3)])
        acc = Rot([take("macc%d" % i, [128, 512]) for i in range(2)])
        def ld_merge(j):
            g__ = wg.next()
            b__ = wb.next()
            for n in range(3):
                load_w(sub(g__, (slice(None), slice(None), slice(n * 128, (n + 1) * 128))), win_d.h[l], 0, 8, 5120 + n * 1024 + j * 128, 128)
                load_w(sub(b__, (slice(None), slice(n * 4, (n + 1) * 4), slice(None))), wbr_d.h[l, n], 0, 4, j * 128, 128)
            return g__, b__
        nxtw = ld_merge(0)
        for j in range(8):
            g_, b_ = nxtw
            if j < 7:
                nxtw = ld_merge(j + 1)
            for (t0, tl) in blocks:
                sl_ = (slice(None), slice(0, tl))
                a_ = acc.next()
                for n in range(3):
                    s_ = sg.next()

                    def ev(pv, t0_, tl_):
                        P.act(sub(s_, sl_), pv, AF.Sigmoid)
                    proj_fm(g_.v(), n * 128, [(t0, tl)], ev)
                    pb = bank_rot.next()
                    for kc in range(4):
                        P.mm(sub(pb, sl_), sub(b_, (slice(None), n * 4 + kc, slice(None))), sub(yT, (slice(None), n * 4 + kc, slice(t0, t0 + tl))),
                             start=(kc == 0), stop=(kc == 3))
                    if n == 0:
                        P.tt("dve", sub(a_, sl_), sub(pb, sl_), sub(s_, sl_), ALU.mult)
                    else:
                        P.tt("dve", sub(s_, sl_), sub(pb, sl_), sub(s_, sl_), ALU.mult)
                        if n == 1:
                            P.tt("pool", sub(a_, sl_), sub(a_, sl_), sub(s_, sl_), ALU.add)
                        else:
                            P.tt("pool", sub(MG, (slice(None), j, slice(t0, t0 + tl))), sub(a_, sl_), sub(s_, sl_), ALU.add)
        P.barrier()
        state["off"] = mark + (8 * NT) // 2
        WO = take("mWO", [128, 8, 1024], BF16)
        load_w(WO.v(), wout_d.h[l], 0, 8, 0, 1024)
        GT = [take("mGT%d" % s, [128, 1024]) for s in range(2)]
        gate_row(GT[0], l, 0, 2)
        if tiles[0] < 2:
            gate_row(GT[1], l, 1, 2)
        hts = Rot([take("mht%d" % i, [128, 1024]) for i in range(3)])
        scrs = Rot([(take("mjunk%d" % i, [128, 1024], BF16), take("mss%d" % i, [128, 4]), take("mhn%d" % i, [128, 1024])) for i in range(3)])
        prevn = None
        tmp = Rot([take("mtmp%d" % i, [128, 512]) for i in range(2)])
        for i in tiles:
            s = 1 if i < 2 else 0
            ht = hts.next()
            P.dma("sp", ht.v(), h_src(l, i))
            for fb in range(2):
                pb = bank_rot.next()
                for kc in range(8):
                    P.mm(pb.v(), sub(MG, (slice(None), kc, slice(i * 128, (i + 1) * 128))), sub(WO, (slice(None), kc, slice(fb * 512, (fb + 1) * 512))),
                         start=(kc == 0), stop=(kc == 7))
                t_ = tmp.next()
                P.tt("dve", t_.v(), pb.v(), sub(GT[s], (slice(None), slice(fb * 512, (fb + 1) * 512))), ALU.mult)
                P.tt("pool", sub(ht, (slice(None), slice(fb * 512, (fb + 1) * 512))), sub(ht, (slice(None), slice(fb * 512, (fb + 1) * 512))), t_.v(), ALU.add)
            P.dma("sp", hbuf.part(i, (slice(i * 128, (i + 1) * 128), slice(None))), ht.v(), sbuf_side=ht.v())
            sc_ = scrs.next()
            norm_tile(ht, i, 1, sc_)
            if prevn is not None:
                norm_tile2(prevn[0], 1, prevn[1])
            prevn = (i, sc_)
        norm_tile2(prevn[0], 1, prevn[1])
        state["off"] = mark

    def ffn_phase(l, tiles, last):
        mark = state["off"]
        state["off"] = PERS - (12 * NT) // 2
        t_first = tiles[0]
        ntl = len(tiles)
        ACC = take("fACC", [128, ntl, 1024])
        HID = take("fHID", [128, 4, NT], BF16)
        W1s = Rot([take("fW1%d" % i, [128, 8, 512], BF16) for i in range(2)])
        W2s = Rot([take("fW2%d" % i, [128, 4, 1024], BF16) for i in range(2)])
        rl = Rot([take("frl%d" % i, [128, 512]) for i in range(3)])
        GT = [take("fGT%d" % s, [128, 1024]) for s in range(2)]
        hts = Rot([take("fht%d" % i, [128, 1024]) for i in range(5)])
        blocks = [b for b in TBLK if b[0] >= t_first * 128]
        def ld_ffn(g):
            w1_ = W1s.next()
            w2_ = W2s.next()
            load_w(w1_.v(), w1_d.h[l], 0, 8, g * 512, 512)
            load_w(w2_.v(), w2_d.h[l], g * 512, 4, 0, 1024)
            return w1_, w2_
        nxtf = ld_ffn(0)
        for g in range(8):
            w1, w2 = nxtf
            if g < 7:
                nxtf = ld_ffn(g + 1)
            for q in range(4):
                def ev(pv, t0, tl):
                    r_ = rl.next()
                    P.act(sub(r_, (slice(None), slice(0, tl))), pv, AF.Relu)
                    P.tt("pool", sub(HID, (slice(None), q, slice(t0, t0 + tl))), sub(r_, (slice(None), slice(0, tl))), sub(r_, (slice(None), slice(0, tl))), ALU.mult)
                proj_fm(w1.v(), q * 128, blocks, ev)
            for ti, i in enumerate(tiles):
                for fb in range(2):
                    pb = bank_rot.next()
                    for q in range(4):
                        P.mm(pb.v(), sub(HID, (slice(None), q, slice(i * 128, (i + 1) * 128))), sub(w2, (slice(None), q, slice(fb * 512, (fb + 1) * 512))),
                             start=(q == 0), stop=(q == 3))
                    av = sub(ACC, (slice(None), ti, slice(fb * 512, (fb + 1) * 512)))
                    if g == 0:
                        P.copy("act", av, pb.v())
                    else:
                        P.tt("dve", av, av, pb.v(), ALU.add)
        gate_row(GT[0], l, 0, 5)
        if t_first < 2:
            gate_row(GT[1], l, 1, 5)
        for ti, i in enumerate(tiles):
            s = 1 if i < 2 else 0
            ht = hts.next()
            hv = hbuf.part(i, (slice(i * 128, (i + 1) * 128), slice(None)))
            P.dma("sp", ht.v(), hv)
            av = sub(ACC, (slice(None), ti, slice(None)))
            P.tt("dve", av, av, GT[s].v(), ALU.mult)
            P.tt("dve", ht.v(), ht.v(), av, ALU.add)
            if last:
                P.dma("sp", V(out_d.h[(i - 2) * 128:(i - 1) * 128, :], (out_d.whole,)), ht.v(), sbuf_side=ht.v())
            else:
                P.dma("sp", hv, ht.v(), sbuf_side=ht.v())
        state["off"] = mark

    def dump_T(name, tn, nchunk):
        d = dbg_tensor(name, [128, nchunk, NT], BF16)
        P.dma("sp", d.v(), tn, sbuf_side=tn)

    stop_after = dbg.get("stop_after")
    for l in range(n_layers):
        last = (l == n_layers - 1)
        modulation(l)
        P.barrier()
        norm1_phase(l)
        P.barrier()
        if dbg.get("u") == l:
            dump_T("dbg_uT", uT.v(), 8)
        if stop_after == ("norm1", l):
            break
        if "A" in dbg.get("mixers", "ABC"):
            mixer_A(l)
            P.barrier()
        if "B" in dbg.get("mixers", "ABC"):
            mixer_B(l)
            P.barrier()
        if "C" in dbg.get("mixers", "ABC"):
            mixer_C(l, do_ctx=not last)
            P.barrier()
        if dbg.get("y") == l:
            dump_T("dbg_yT", yT.v(), 12)
        if stop_after == ("mix", l):
            break
        if last:
            blocks, tiles = TBLK[1:], list(range(2, NTILE))
        else:
            blocks, tiles = TBLK, list(range(NTILE))
        merge_phase(l, blocks, tiles)
        P.barrier()
        ffn_phase(l, tiles, last)
        P.barrier()

    final = []
    seenb = set()
    for e in ENGS:
        for waits, fn, tok in P.ops[e]:
            if not isinstance(tok[0], str) and id(tok[0]) not in seenb:
                seenb.add(id(tok[0]))
                final.append(tok)
    n = P.finish(final_waits=final)
    return nc, n, list(dbg_out.keys())


def _consts():
    c = {}
    c["ident"] = np.eye(128, dtype=np.float32)
    t = np.arange(2048)
    inv = (10000.0 ** (-np.arange(16, dtype=np.float32) / 16)).astype(np.float32)
    cs = np.zeros((2, 128, 2048), np.float32)
    for pp in range(128):
        d = pp % 64
        pos = (t // 64) if d < 32 else (t % 64)
        ang = pos.astype(np.float32) * inv[d % 16]
        cs[0, pp] = np.cos(ang)
        cs[1, pp] = np.sin(ang)
    c["rope_cs"] = cs
    R = np.zeros((128, 128), np.float32)
    for m in range(128):
        i = m % 32
        if i < 16:
            R[m, m + 16] = -1.0
        else:
            R[m, m - 16] = 1.0
    c["rotT"] = np.ascontiguousarray(R.T)
    b = np.zeros((128, 128), np.float32)
    b[:64, :64] = 1.0 / 64
    b[64:, 64:] = 1.0 / 64
    c["blk"] = b
    rm = np.ones((128, 2305), np.float32)
    rm[:, ::32] = 0.0
    c["rmask"] = rm
    rm64 = np.ones((128, 2305), np.float32)
    rm64[:, ::64] = 0.0
    c["rmask64"] = rm64
    s = np.arange(64)[:, None]
    tt = np.arange(64)[None, :]
    c["tri"] = np.stack([(s <= tt), (s >= tt)]).astype(np.float32)
    return c


def _nab(rpb):
    out = np.full((2, 8, 21, 128, 128), -1e30, np.float32)
    a = np.arange(128)[:, None] // 64
    qc = np.arange(128)[:, None] % 64
    b = np.arange(128)[None, :] // 64
    kc = np.arange(128)[None, :] % 64
    cstart = np.clip(qc - 8, 0, 48)
    col_ok = (kc >= cstart) & (kc < cstart + 16)
    dcol = np.clip(kc - qc, -15, 15) + 15
    classes = [(8, [(2 + j, j) for j in range(5)])]
    classes.append((0, [(j, 5 + j) for j in range(4)]))
    classes.append((2, [(j, 9 + j) for j in range(4)]))
    classes.append((28, [(12 + j, 13 + j) for j in range(4)]))
    classes.append((30, [(12 + j, 17 + j) for j in range(4)]))
    for r, lst in classes:
        for kt, mi in lst:
            qrow = r + a
            krow = 2 * kt + b
            kr0 = np.clip(qrow - 4, 0, 24)
            valid = (krow >= kr0) & (krow < kr0 + 8) & col_ok
            drow = np.clip(krow - qrow + 7, 0, 14)
            drow, dcol_b = np.broadcast_arrays(drow, dcol)
            g = rpb[:, :, drow, dcol_b]
            out[:, :, mi] = np.swapaxes(np.where(valid[None, None], g, np.float32(-1e30)), -1, -2)
    return out


def _prep(inp):
    f = np.float32
    shared = {}
    for k in ("ada_w", "w_in", "w_branch", "w_out", "ffn_w1", "ffn_w2"):
        shared[k] = np.ascontiguousarray(inp[k], dtype=f)
    shared["adabT"] = np.ascontiguousarray(inp["ada_b"].reshape(2, 48, 128).transpose(0, 2, 1))
    shared["g1T"] = np.ascontiguousarray(inp["norm1_g"].reshape(2, 8, 128).transpose(0, 2, 1))
    shared["g2T"] = np.ascontiguousarray(inp["norm2_g"].reshape(2, 8, 128).transpose(0, 2, 1))
    shared["convT"] = np.ascontiguousarray(inp["conv_w"].reshape(2, 4, 4, 128).transpose(0, 3, 2, 1).reshape(2, 128, 16))
    shared["convb"] = np.ascontiguousarray(inp["conv_b"].reshape(2, 4, 128).transpose(0, 2, 1))
    for name, key in (("bda", "lru_wa"), ("bdx", "lru_wx")):
        w = inp[key]
        bd = np.zeros((2, 2, 4, 128, 128), f)
        for cch in range(4):
            for hf in range(2):
                bd[:, :, cch, hf * 64:(hf + 1) * 64, hf * 64:(hf + 1) * 64] = w[:, :, 2 * cch + hf]
        shared[name] = bd
    lv = np.stack([inp["lru_ba"], inp["lru_bx"], inp["lru_lambda"]], axis=1)
    shared["lruv"] = np.ascontiguousarray(lv.reshape(2, 3, 2, 4, 128).transpose(0, 4, 1, 2, 3).reshape(2, 128, 24))
    lb = inp["hg_lb_logits"]
    shared["lbl"] = np.ascontiguousarray(lb.reshape(2, 2, 4, 128).transpose(3, 0, 1, 2).reshape(128, 16))
    shared["hgg"] = np.ascontiguousarray(inp["hg_norm_g"], dtype=f)
    ng = np.stack([np.tile(inp["na_q_norm_g"], (1, 2)), np.tile(inp["na_k_norm_g"], (1, 2))], axis=2)
    shared["nag"] = np.ascontiguousarray(ng, dtype=f)
    shared["nab"] = _nab(np.asarray(inp["na_rpb"], dtype=f))
    shared.update(_consts())
    maps = []
    for b in range(8):
        m = dict(shared)
        m["x"] = np.ascontiguousarray(inp["x"][b], dtype=f)
        m["ctx"] = np.ascontiguousarray(inp["ctx"][b], dtype=f)
        cc = np.stack([inp["c"][b].reshape(8, 128).T, inp["c_ctx"].reshape(8, 128).T], axis=2)
        m["cc"] = np.ascontiguousarray(cc.reshape(128, 16), dtype=f)
        maps.append(m)
    return maps


_CACHE = {}


def kernel(**inputs):
    inp = {k: np.asarray(v) for k, v in inputs.items()}
    if "nc" not in _CACHE:
        _CACHE["nc"] = build()[0]
    nc = _CACHE["nc"]
    maps = _prep(inp)
    res = run_bass_kernel_spmd(nc, maps, core_ids=list(range(8)))
    out = np.stack([np.asarray(r["out"], dtype=np.float32) for r in res.results], axis=0)
    return out
```
